# Optimizing a Trainium2 kernel written in Bass

```python
import jax
import jax.numpy as jnp
from jax import lax
import numpy as np

D_MODEL = 2048
BATCH = 4
SEQ = 8192
DEPTH = 4

MEM_LEN = 256
HEAD_DIM = 128
ROPE_THETA = 500000.0
ROPE_DIM = HEAD_DIM // 4
EPS = 1e-6
NEG_INF = -1e30
D_FF = 5632

GDN_HEADS = 8
GDN_DK = 128
GDN_DV = 128
GDN_QK_W = GDN_HEADS * GDN_DK
GDN_V_W = GDN_HEADS * GDN_DV
GDN_QKV_W = 2 * GDN_QK_W + GDN_V_W
GDN_CONV = 4
GDN_CHUNK = 64

DSW_HEADS = 8
DSW_W = DSW_HEADS * HEAD_DIM
DSW_PAIRS = ((128, 1), (512, 4), (2048, 16))

EVEN_IN_W = GDN_QKV_W + GDN_V_W + 2 * GDN_HEADS + 3 * DSW_W
EVEN_MIX_W = GDN_V_W + DSW_W

NSA_HEADS = 16
NSA_KV_HEADS = 4
NSA_Q_W = NSA_HEADS * HEAD_DIM
NSA_KV_W = NSA_KV_HEADS * HEAD_DIM
NSA_CMP_BLOCK = 32
NSA_CMP_STRIDE = 16
NSA_CMP_HIDDEN = 256
NSA_SLC_BLOCK = 64
NSA_SLC_TOPK = 16
NSA_WINDOW = 512
NSA_Q_BLOCK = 64
NSA_FORCE_BONUS = 1e3
ODD_IN_W = NSA_Q_W + 6 * NSA_KV_W + 3 * NSA_HEADS

XA_HEADS = 4
XA_W = XA_HEADS * HEAD_DIM

kernel_name = 'hybrid_gdn_dilated_nsa_trunk'


def rms_norm(x, g):
    xf = x.astype(jnp.float32)
    y = xf * lax.rsqrt(jnp.mean(xf * xf, axis=-1, keepdims=True) + EPS)
    return (y * g.astype(jnp.float32)).astype(x.dtype)


def l2_normalize(x):
    return x * lax.rsqrt(jnp.sum(x * x, axis=-1, keepdims=True) + EPS)


def rope_partial(x, positions):
    half = ROPE_DIM // 2
    inv_freq = jnp.float32(ROPE_THETA) ** (-jnp.arange(half, dtype=jnp.float32) / half)
    ang = positions.astype(jnp.float32)[:, :, None, None] * inv_freq
    cos, sin = jnp.cos(ang), jnp.sin(ang)
    x1 = x[..., :half].astype(jnp.float32)
    x2 = x[..., half:ROPE_DIM].astype(jnp.float32)
    rot = jnp.concatenate([x1 * cos - x2 * sin, x2 * cos + x1 * sin], axis=-1).astype(x.dtype)
    return jnp.concatenate([rot, x[..., ROPE_DIM:]], axis=-1)


def swiglu(x, w_gu, w_down):
    gate, up = jnp.split(x @ w_gu, 2, axis=-1)
    return (jax.nn.silu(gate) * up) @ w_down


def masked_softmax(s, valid):
    s = jnp.where(valid, s, NEG_INF)
    m = jnp.max(s, axis=-1, keepdims=True)
    p = jnp.where(valid, jnp.exp(s - m), 0.0)
    return p / jnp.maximum(jnp.sum(p, axis=-1, keepdims=True), 1e-30)


def causal_depthwise_conv(x, w):
    k_len = w.shape[0]
    return lax.conv_general_dilated(x, w[:, None, :], window_strides=(1,), padding=((k_len - 1, 0),),
                                    dimension_numbers=('NWC', 'WIO', 'NWC'),
                                    feature_group_count=x.shape[-1])


def gated_delta_rule(q, k, v, g, beta):
    b_, s_, h_, dk = q.shape
    dv = v.shape[-1]
    c = GDN_CHUNK
    n = s_ // c
    q = l2_normalize(q.astype(jnp.float32)) * (dk ** -0.5)
    k = l2_normalize(k.astype(jnp.float32))

    def to_chunks(t):
        return t.reshape(b_, n, c, h_, -1).transpose(0, 3, 1, 2, 4)
    q, k, v = to_chunks(q), to_chunks(k), to_chunks(v.astype(jnp.float32))
    g = g.astype(jnp.float32).reshape(b_, n, c, h_).transpose(0, 3, 1, 2)
    beta = beta.astype(jnp.float32).reshape(b_, n, c, h_).transpose(0, 3, 1, 2)
    gam = jnp.cumsum(g, axis=-1)
    causal = jnp.tril(jnp.ones((c, c), bool))
    strict = jnp.tril(jnp.ones((c, c), bool), -1)
    diff = gam[..., :, None] - gam[..., None, :]
    decay = jnp.where(causal, jnp.exp(jnp.where(causal, diff, 0.0)), 0.0)
    kb = k * beta[..., None]
    lower = jnp.where(strict, jnp.einsum('bhnid,bhnjd->bhnij', kb, k) * decay, 0.0)
    a_mat = lower + jnp.eye(c, dtype=jnp.float32)
    rhs = jnp.concatenate([v * beta[..., None], kb * jnp.exp(gam)[..., None]], axis=-1)
    sol = lax.linalg.triangular_solve(a_mat, rhs, left_side=True, lower=True, unit_diagonal=True)
    u, w = sol[..., :dv], sol[..., dv:]
    qk = jnp.where(causal, jnp.einsum('bhnid,bhnjd->bhnij', q, k) * decay, 0.0)
    q_dec = q * jnp.exp(gam)[..., None]
    k_dec = k * jnp.exp(gam[..., -1:] - gam)[..., None]
    chunk_decay = jnp.exp(gam[..., -1])

    def step(state, xs):
        qk_n, qd_n, kd_n, u_n, w_n, cd_n = xs
        v_new = u_n - jnp.einsum('bhcd,bhde->bhce', w_n, state)
        out = jnp.einsum('bhcd,bhde->bhce', qd_n, state) + jnp.einsum('bhij,bhje->bhie', qk_n, v_new)
        state = state * cd_n[..., None, None] + jnp.einsum('bhcd,bhce->bhde', kd_n, v_new)
        return state, out

    xs = tuple(jnp.moveaxis(t, 2, 0) for t in (qk, q_dec, k_dec, u, w, chunk_decay))
    state0 = jnp.zeros((b_, h_, dk, dv), jnp.float32)
    _, out = lax.scan(step, state0, xs)
    return out.transpose(1, 0, 3, 2, 4).reshape(b_, s_, h_, dv)


def dilated_branch(q, k, v, window, dil):
    b_, s_, h_, hd = q.shape
    span = window // dil
    unit = dil * span
    s_pad = -(-s_ // unit) * unit
    m_len = s_pad // dil
    nb = m_len // span

    def split(t):
        t = jnp.pad(t, ((0, 0), (0, s_pad - s_), (0, 0), (0, 0)))
        return t.reshape(b_, m_len, dil, h_, hd).transpose(0, 2, 3, 1, 4).reshape(b_, dil, h_, nb, span, hd)

    def with_prev(t):
        prev = jnp.pad(t, ((0, 0), (0, 0), (0, 0), (1, 0), (0, 0), (0, 0)))[:, :, :, :-1]
        return jnp.concatenate([prev, t], axis=4)

    qb, kb, vb = split(q), split(k), split(v)
    kk, vv = with_prev(kb), with_prev(vb)
    s = jnp.einsum('brhnqd,brhnkd->brhnqk', qb, kk, preferred_element_type=jnp.float32) * (hd ** -0.5)
    i = jnp.arange(span)[:, None]
    j = jnp.arange(2 * span)[None, :]
    dist = span + i - j
    band = (dist >= 0) & (dist <= span)
    first = (jnp.arange(nb) == 0)[:, None, None] & (j < span)[None]
    valid = band[None] & ~first
    s = jnp.where(valid, s, NEG_INF)
    m = jnp.max(s, axis=-1, keepdims=True)
    p = jnp.where(valid, jnp.exp(s - m), 0.0)
    l = jnp.sum(p, axis=-1, keepdims=True)
    num = jnp.einsum('brhnqk,brhnkd->brhnqd', p, vv.astype(jnp.float32))

    def merge(t):
        e = t.shape[-1]
        return t.reshape(b_, dil, h_, m_len, e).transpose(0, 3, 1, 2, 4).reshape(b_, s_pad, h_, e)[:, :s_]
    return merge(num), merge(m), merge(l)


def dilated_attention(q, k, v):
    outs = [dilated_branch(q, k, v, w, d) for (w, d) in DSW_PAIRS]
    m_all = jnp.max(jnp.stack([o[1] for o in outs]), axis=0)
    num = sum(jnp.exp(o[1] - m_all) * o[0] for o in outs)
    den = sum(jnp.exp(o[1] - m_all) * o[2] for o in outs)
    return (num / den).astype(q.dtype)


def compress_blocks(t, pos, w1, w2):
    b_, s_, g_, hd = t.shape
    n_cmp = (s_ - NSA_CMP_BLOCK) // NSA_CMP_STRIDE + 1
    idx = np.arange(n_cmp)[:, None] * NSA_CMP_STRIDE + np.arange(NSA_CMP_BLOCK)[None, :]
    blk = t[:, idx].transpose(0, 3, 1, 2, 4) + pos
    flat = blk.reshape(b_, g_, n_cmp, NSA_CMP_BLOCK * hd)
    return jax.nn.silu(flat @ w1) @ w2


def nsa_attention(q, kc, vc, ks, vs, kw, vw, gates):
    b_, s_, hq, hd = q.shape
    g_ = ks.shape[2]
    hpg = hq // g_
    qbl, win, ls = NSA_Q_BLOCK, NSA_WINDOW, NSA_SLC_BLOCK
    n_cmp = kc.shape[2]
    n_slc = s_ // ls
    topk = min(NSA_SLC_TOPK, n_slc)
    scale = hd ** -0.5
    qg = q.reshape(b_, s_, g_, hpg, hd)
    gg = gates.reshape(b_, s_, g_, hpg, 3)
    ks_b = ks.transpose(0, 2, 1, 3).reshape(b_, g_, n_slc, ls, hd)
    vs_b = vs.transpose(0, 2, 1, 3).reshape(b_, g_, n_slc, ls, hd)
    kw_p = jnp.pad(kw.transpose(0, 2, 1, 3), ((0, 0), (0, 0), (win, 0), (0, 0)))
    vw_p = jnp.pad(vw.transpose(0, 2, 1, 3), ((0, 0), (0, 0), (win, 0), (0, 0)))
    cmp_end = np.arange(n_cmp, dtype=np.int32) * NSA_CMP_STRIDE + (NSA_CMP_BLOCK - 1)
    c_start = np.arange(n_cmp)[:, None] * NSA_CMP_STRIDE
    s_start = np.arange(n_slc)[None, :] * ls
    overlap = jnp.asarray((c_start < s_start + ls) & (c_start + NSA_CMP_BLOCK > s_start), jnp.float32)
    b_idx = jnp.arange(b_)[:, None, None, None]
    g_idx = jnp.arange(g_)[None, :, None, None]
    blk_ids = jnp.arange(n_slc)

    def block(nb):
        t0 = nb * qbl
        tpos = t0 + jnp.arange(qbl)
        qb = lax.dynamic_slice_in_dim(qg, t0, qbl, axis=1)
        gb = lax.dynamic_slice_in_dim(gg, t0, qbl, axis=1)
        s_c = jnp.einsum('bqghd,bgnd->bgqhn', qb, kc, preferred_element_type=jnp.float32) * scale
        valid_c = (cmp_end[None, :] <= tpos[:, None])[None, None, :, None, :]
        p_c = masked_softmax(s_c, valid_c)
        o_c = jnp.einsum('bgqhn,bgnd->bqghd', p_c, vc)
        imp = jnp.einsum('bgqhn,nj->bgqj', p_c, overlap)
        cur = (tpos // ls)[:, None]
        allowed = blk_ids[None, :] * ls <= tpos[:, None]
        forced = (blk_ids[None, :] == 0) | (blk_ids[None, :] == cur) | (blk_ids[None, :] == cur - 1)
        score = jnp.where(allowed, imp + jnp.where(forced, NSA_FORCE_BONUS, 0.0), NEG_INF)
        _, sel = lax.top_k(score, topk)
        k_sel = ks_b[b_idx, g_idx, sel].reshape(b_, g_, qbl, topk * ls, hd)
        v_sel = vs_b[b_idx, g_idx, sel].reshape(b_, g_, qbl, topk * ls, hd)
        tok = (sel[..., None] * ls + jnp.arange(ls)).reshape(b_, g_, qbl, topk * ls)
        s_s = jnp.einsum('bqghd,bgqkd->bgqhk', qb, k_sel, preferred_element_type=jnp.float32) * scale
        valid_s = (tok <= tpos[None, None, :, None])[:, :, :, None, :]
        p_s = masked_softmax(s_s, valid_s)
        o_s = jnp.einsum('bgqhk,bgqkd->bqghd', p_s, v_sel)
        k_w = lax.dynamic_slice_in_dim(kw_p, t0, win + qbl, axis=2)
        v_w = lax.dynamic_slice_in_dim(vw_p, t0, win + qbl, axis=2)
        s_w = jnp.einsum('bqghd,bgkd->bgqhk', qb, k_w, preferred_element_type=jnp.float32) * scale
        jj = jnp.arange(win + qbl)[None, :]
        dist = win + jnp.arange(qbl)[:, None] - jj
        valid_w = ((dist >= 0) & (dist < win) & (jj >= win - t0))[None, None, :, None, :]
        p_w = masked_softmax(s_w, valid_w)
        o_w = jnp.einsum('bgqhk,bgkd->bqghd', p_w, v_w)
        out = gb[..., 0:1] * o_c + gb[..., 1:2] * o_s + gb[..., 2:3] * o_w
        return out.astype(q.dtype)

    outs = lax.map(block, jnp.arange(s_ // qbl))
    return outs.transpose(1, 0, 2, 3, 4, 5).reshape(b_, s_, hq, hd)


def even_mixer(h, positions, w_in, w_out, conv_w, a_log, dt_bias, gdn_norm, q_norm, k_norm):
    b_, s_, _ = h.shape
    proj = h @ w_in
    sizes = [GDN_QKV_W, GDN_V_W, GDN_HEADS, GDN_HEADS, DSW_W, DSW_W, DSW_W]
    qkv_a, z, b_logit, a_logit, q_b, k_b, v_b = jnp.split(proj, np.cumsum(sizes)[:-1].tolist(), axis=-1)

    def heads(t, d):
        return t.reshape(b_, s_, -1, d)
    qkv_a = jax.nn.silu(causal_depthwise_conv(qkv_a, conv_w))
    q_a, k_a, v_a = jnp.split(qkv_a, [GDN_QK_W, 2 * GDN_QK_W], axis=-1)
    beta = jax.nn.sigmoid(b_logit.astype(jnp.float32))
    g = -jnp.exp(a_log.astype(jnp.float32)) * jax.nn.softplus(a_logit.astype(jnp.float32) + dt_bias.astype(jnp.float32))
    o_a = gated_delta_rule(heads(q_a, GDN_DK), heads(k_a, GDN_DK), heads(v_a, GDN_DV), g, beta).astype(h.dtype)
    o_a = rms_norm(o_a, gdn_norm) * jax.nn.silu(heads(z, GDN_DV))
    qh = rope_partial(rms_norm(heads(q_b, HEAD_DIM), q_norm), positions)
    kh = rope_partial(rms_norm(heads(k_b, HEAD_DIM), k_norm), positions)
    o_b = dilated_attention(qh, kh, heads(v_b, HEAD_DIM))
    mixed = jnp.concatenate([o_a.reshape(b_, s_, -1), o_b.reshape(b_, s_, -1)], axis=-1)
    return mixed @ w_out


def odd_mixer(h, positions, w_in, w_out, q_norm, k_norm, cmp_pos, cmp_w1, cmp_w2):
    b_, s_, _ = h.shape
    proj = h @ w_in
    sizes = [NSA_Q_W] + [NSA_KV_W] * 6 + [3 * NSA_HEADS]
    q, k_c, v_c, k_s, v_s, k_w, v_w, gate_logits = jnp.split(proj, np.cumsum(sizes)[:-1].tolist(), axis=-1)

    def heads(t):
        return t.reshape(b_, s_, -1, HEAD_DIM)
    q = rope_partial(rms_norm(heads(q), q_norm), positions)
    k_c = rope_partial(rms_norm(heads(k_c), k_norm[0]), positions)
    k_s = rope_partial(rms_norm(heads(k_s), k_norm[1]), positions)
    k_w = rope_partial(rms_norm(heads(k_w), k_norm[2]), positions)
    kc_blk = compress_blocks(k_c, cmp_pos[0], cmp_w1[0], cmp_w2[0])
    vc_blk = compress_blocks(heads(v_c), cmp_pos[1], cmp_w1[1], cmp_w2[1])
    gates = jax.nn.sigmoid(gate_logits.astype(jnp.float32)).reshape(b_, s_, NSA_HEADS, 3)
    o = nsa_attention(q, kc_blk, vc_blk, k_s, heads(v_s), k_w, heads(v_w), gates)
    return o.reshape(b_, s_, -1) @ w_out


def cross_attention(h, m, w_q, w_kv, q_norm, k_norm, w_o):
    b_, s_, _ = h.shape
    m_len = m.shape[1]
    q = rms_norm((h @ w_q).reshape(b_, s_, XA_HEADS, HEAD_DIM), q_norm)
    kv = (m @ w_kv).reshape(b_, m_len, 2, XA_HEADS, HEAD_DIM)
    k = rms_norm(kv[:, :, 0], k_norm)
    v = kv[:, :, 1]
    s = jnp.einsum('bshd,bmhd->bhsm', q, k, preferred_element_type=jnp.float32) * (HEAD_DIM ** -0.5)
    p = jax.nn.softmax(s, axis=-1)
    o = jnp.einsum('bhsm,bmhd->bshd', p, v).astype(h.dtype)
    return o.reshape(b_, s_, -1) @ w_o


def setup_inputs(seed: int = 0) -> dict:
    key = jax.random.key(seed)
    keys = iter(jax.random.split(key, 64))
    f32 = jnp.float32
    n_even = (DEPTH + 1) // 2
    n_odd = DEPTH // 2

    def dense(shape, fan_in):
        return jax.random.normal(next(keys), shape, f32) * (fan_in ** -0.5)

    def gain(shape):
        return 1.0 + 0.02 * jax.random.normal(next(keys), shape, f32)

    x = jax.random.normal(next(keys), (BATCH, SEQ, D_MODEL), f32)
    mem = jax.random.normal(next(keys), (BATCH, MEM_LEN, D_MODEL), f32)
    positions = jnp.broadcast_to(jnp.arange(SEQ, dtype=jnp.int32), (BATCH, SEQ))
    dt = jnp.exp(jax.random.uniform(next(keys), (n_even, GDN_HEADS), f32, minval=float(np.log(1e-3)), maxval=float(np.log(1e-1))))
    return {
        'x': x,
        'mem': mem,
        'positions': positions,
        'ffn1_norm': gain((DEPTH, D_MODEL)),
        'ffn1_w_gu': dense((DEPTH, D_MODEL, 2 * D_FF), D_MODEL),
        'ffn1_w_down': dense((DEPTH, D_FF, D_MODEL), D_FF),
        'mix_norm': gain((DEPTH, D_MODEL)),
        'ev_w_in': dense((n_even, D_MODEL, EVEN_IN_W), D_MODEL),
        'ev_w_out': dense((n_even, EVEN_MIX_W, D_MODEL), EVEN_MIX_W),
        'gdn_conv_w': dense((n_even, GDN_CONV, GDN_QKV_W), GDN_CONV),
        'gdn_a_log': jnp.log(jax.random.uniform(next(keys), (n_even, GDN_HEADS), f32, minval=1.0, maxval=16.0)),
        'gdn_dt_bias': dt + jnp.log(-jnp.expm1(-dt)),
        'gdn_out_norm': gain((n_even, GDN_DV)),
        'dsw_q_norm': gain((n_even, HEAD_DIM)),
        'dsw_k_norm': gain((n_even, HEAD_DIM)),
        'od_w_in': dense((n_odd, D_MODEL, ODD_IN_W), D_MODEL),
        'od_w_out': dense((n_odd, NSA_Q_W, D_MODEL), NSA_Q_W),
        'nsa_q_norm': gain((n_odd, HEAD_DIM)),
        'nsa_k_norm': gain((n_odd, 3, HEAD_DIM)),
        'nsa_cmp_pos': 0.02 * jax.random.normal(next(keys), (n_odd, 2, NSA_CMP_BLOCK, HEAD_DIM), f32),
        'nsa_cmp_w1': dense((n_odd, 2, NSA_CMP_BLOCK * HEAD_DIM, NSA_CMP_HIDDEN), NSA_CMP_BLOCK * HEAD_DIM),
        'nsa_cmp_w2': dense((n_odd, 2, NSA_CMP_HIDDEN, HEAD_DIM), NSA_CMP_HIDDEN),
        'xa_norm': gain((DEPTH, D_MODEL)),
        'xa_mem_norm': gain((DEPTH, D_MODEL)),
        'xa_w_q': dense((DEPTH, D_MODEL, XA_W), D_MODEL),
        'xa_w_kv': dense((DEPTH, D_MODEL, 2 * XA_W), D_MODEL),
        'xa_q_norm': gain((DEPTH, HEAD_DIM)),
        'xa_k_norm': gain((DEPTH, HEAD_DIM)),
        'xa_w_o': dense((DEPTH, XA_W, D_MODEL), XA_W),
        'ffn2_norm': gain((DEPTH, D_MODEL)),
        'ffn2_w_gu': dense((DEPTH, D_MODEL, 2 * D_FF), D_MODEL),
        'ffn2_w_down': dense((DEPTH, D_FF, D_MODEL), D_FF),
    }


def reference(x, mem, positions, ffn1_norm, ffn1_w_gu, ffn1_w_down, mix_norm,
              ev_w_in, ev_w_out, gdn_conv_w, gdn_a_log, gdn_dt_bias, gdn_out_norm, dsw_q_norm, dsw_k_norm,
              od_w_in, od_w_out, nsa_q_norm, nsa_k_norm, nsa_cmp_pos, nsa_cmp_w1, nsa_cmp_w2,
              xa_norm, xa_mem_norm, xa_w_q, xa_w_kv, xa_q_norm, xa_k_norm, xa_w_o,
              ffn2_norm, ffn2_w_gu, ffn2_w_down):
    for i in range(DEPTH):
        x = x + 0.5 * swiglu(rms_norm(x, ffn1_norm[i]), ffn1_w_gu[i], ffn1_w_down[i])
        h = rms_norm(x, mix_norm[i])
        if i % 2 == 0:
            e = i // 2
            x = x + even_mixer(h, positions, ev_w_in[e], ev_w_out[e], gdn_conv_w[e], gdn_a_log[e],
                               gdn_dt_bias[e], gdn_out_norm[e], dsw_q_norm[e], dsw_k_norm[e])
        else:
            o = i // 2
            x = x + odd_mixer(h, positions, od_w_in[o], od_w_out[o], nsa_q_norm[o], nsa_k_norm[o],
                              nsa_cmp_pos[o], nsa_cmp_w1[o], nsa_cmp_w2[o])
        x = x + cross_attention(rms_norm(x, xa_norm[i]), rms_norm(mem, xa_mem_norm[i]), xa_w_q[i], xa_w_kv[i],
                                xa_q_norm[i], xa_k_norm[i], xa_w_o[i])
        x = x + 0.5 * swiglu(rms_norm(x, ffn2_norm[i]), ffn2_w_gu[i], ffn2_w_down[i])
    return x
```

```python
import numpy as np
from contextlib import ExitStack
import concourse.bass as bass
import concourse.mybir as mybir
from concourse.bass_utils import run_bass_kernel_spmd

F32 = mybir.dt.float32
BF16 = mybir.dt.bfloat16
I32 = mybir.dt.int32
ALU = mybir.AluOpType
AF = mybir.ActivationFunctionType
AX = mybir.AxisListType


class Sched:
    ENGS = ['pe', 'act', 'dve', 'pool', 'sp']

    def __init__(self, nc, es, n_sp=40, n_pool=24):
        self.nc = nc
        self.es = es
        self.ops = {e: [] for e in self.ENGS}
        self.cnt = {e: 0 for e in self.ENGS}
        self.waited = {e: {} for e in self.ENGS}
        self.lastw = {}
        self.readers = {}
        self.sems = {}
        for e in self.ENGS[:4]:
            self.sems[e] = es.enter_context(nc.semaphore('s_' + e))
        self.rings = {}
        for q, n in (('sp', n_sp), ('pool', n_pool)):
            self.rings[q] = {'n': n, 'i': 0, 'uses': [0] * n}
            for k in range(n):
                self.sems[(q, k)] = es.enter_context(nc.semaphore('d_%s_%d' % (q, k)))
        self.nt = 0
        self.gran = {}
        self.psum_names = set()
        self.pstep = {}

    def sb(self, shape, dtype, name=None):
        self.nt += 1
        return self.es.enter_context(self.nc.sbuf_tensor(name or ('sb%d' % self.nt), list(shape), dtype))

    def ps(self, shape, dtype=F32, name=None):
        self.nt += 1
        self.psum_names.add(name or ('ps%d' % self.nt))
        return self.es.enter_context(self.nc.psum_tensor(name or ('ps%d' % self.nt), list(shape), dtype))

    def keys(self, x):
        if isinstance(x, (str, tuple)):
            return [x]
        t = getattr(x, 'tensor', x)
        name = t.name
        g = self.gran.get(name)
        if g is None or not hasattr(x, 'ap'):
            return [name]
        pat = list(x.ap)
        pstep = self.pstep[name]
        lo = x.offset % pstep
        hi = lo + sum((c - 1) * st for st, c in pat[1:]) + 1
        return [(name, i) for i in range(lo // g, (hi - 1) // g + 1)]

    def set_gran(self, t, g):
        n = 1
        for d in t.shape[1:]:
            n *= d
        self.gran[t.name] = g
        self.pstep[t.name] = n

    def _klist(self, xs):
        out = []
        for x in xs:
            out.extend(self.keys(x))
        return out

    def _deps(self, eng, reads, writes):
        deps = {}

        def add(sk, v):
            if deps.get(sk, 0) < v:
                deps[sk] = v
        for r in reads:
            ev = self.lastw.get(r)
            if ev is not None:
                add(*ev)
        for w in writes:
            ev = self.lastw.get(w)
            if ev is not None:
                add(*ev)
            for sk, v in self.readers.get(w, {}).items():
                add(sk, v)
        waits = []
        for sk, v in deps.items():
            if sk == eng and eng == 'pe':
                continue
            if self.waited[eng].get(sk, 0) >= v:
                continue
            self.waited[eng][sk] = v
            waits.append((sk, v))
        return waits

    def _commit(self, ev, reads, writes):
        for r in reads:
            d = self.readers.setdefault(r, {})
            if d.get(ev[0], 0) < ev[1]:
                d[ev[0]] = ev[1]
        for w in writes:
            self.lastw[w] = ev
            self.readers[w] = {}

    def op(self, eng, fn, reads=(), writes=()):
        reads = self._klist(reads)
        writes = self._klist(writes)
        writes = writes + [r for r in reads if (r if isinstance(r, str) else r[0]) in self.psum_names and r not in writes]
        waits = self._deps(eng, reads, writes)
        self.cnt[eng] += 1
        ev = (eng, self.cnt[eng])
        self.ops[eng].append((waits, fn, ev, 1))
        self._commit(ev, reads, writes)

    def dma(self, q, out, in_, reads=None, writes=None, **kw):
        reads = self._klist(reads if reads is not None else [in_])
        writes = self._klist(writes if writes is not None else [out])
        ring = self.rings[q]
        k = ring['i'] % ring['n']
        ring['i'] += 1
        sk = (q, k)
        waits = self._deps(q, reads, writes)
        prev = ring['uses'][k]
        if prev > 0 and self.waited[q].get(sk, 0) < 16 * prev:
            self.waited[q][sk] = 16 * prev
            waits.append((sk, 16 * prev))
        ring['uses'][k] += 1
        ev = (sk, 16 * ring['uses'][k])
        self.ops[q].append((waits, (lambda e: e.dma_start(out=out, in_=in_, **kw)), ev, 16))
        self._commit(ev, reads, writes)

    def finish(self):
        waits = []
        for q, ring in self.rings.items():
            for k in range(ring['n']):
                if ring['uses'][k] > 0:
                    waits.append(((q, k), 16 * ring['uses'][k]))
        for e in self.ENGS[:4]:
            if self.cnt[e] > 0:
                waits.append((e, self.cnt[e]))
        self.ops['sp'].append((waits, None, None, 0))

    def emit(self):
        nc = self.nc
        sems = self.sems

        def replay(name, e):
            for waits, fn, ev, inc in self.ops[name]:
                for sk, v in waits:
                    e.wait_ge(sems[sk], v)
                if fn is not None:
                    ins = fn(e)
                    ins.then_inc(sems[ev[0]], inc)
        with nc.Block() as block:
            @block.tensor
            def _(e):
                replay('pe', e)

            @block.scalar
            def _(e):
                replay('act', e)

            @block.vector
            def _(e):
                replay('dve', e)

            @block.gpsimd
            def _(e):
                replay('pool', e)

            @block.sync
            def _(e):
                replay('sp', e)


def _aps(*xs):
    return [x for x in xs if x is not None and not isinstance(x, (int, float))]


def _add_helpers():
    def mm(self, out, lhsT, rhs, start=True, stop=True):
        self.op('pe', lambda e: e.matmul(out, lhsT=lhsT, rhs=rhs, start=start, stop=stop), reads=[lhsT, rhs], writes=[out])

    def tr(self, out, in_, ident):
        self.op('pe', lambda e: e.transpose(out=out, in_=in_, identity=ident), reads=[in_, ident], writes=[out])

    def act(self, out, in_, func, scale=1.0, bias=0.0):
        self.op('act', lambda e: e.activation(out=out, in_=in_, func=func, scale=scale, bias=bias),
                reads=_aps(in_, scale, bias), writes=[out])

    def tt(self, eng, out, in0, in1, op):
        self.op(eng, lambda e: e.tensor_tensor(out=out, in0=in0, in1=in1, op=op), reads=[in0, in1], writes=[out])

    def stt(self, eng, out, in0, scalar, in1, op0, op1):
        self.op(eng, lambda e: e.scalar_tensor_tensor(out=out, in0=in0, scalar=scalar, in1=in1, op0=op0, op1=op1),
                reads=_aps(in0, scalar, in1), writes=[out])

    def ts(self, eng, out, in0, s1, s2=None, op0=ALU.mult, op1=None):
        if op1 is None:
            self.op(eng, lambda e: e.tensor_scalar(out=out, in0=in0, scalar1=s1, scalar2=None, op0=op0),
                    reads=_aps(in0, s1), writes=[out])
        else:
            self.op(eng, lambda e: e.tensor_scalar(out=out, in0=in0, scalar1=s1, scalar2=s2, op0=op0, op1=op1),
                    reads=_aps(in0, s1, s2), writes=[out])

    def copy(self, eng, out, in_):
        if eng == 'act':
            self.op('act', lambda e: e.copy(out=out, in_=in_), reads=[in_], writes=[out])
        else:
            self.op(eng, lambda e: e.tensor_copy(out=out, in_=in_), reads=[in_], writes=[out])

    def memset(self, eng, out, val):
        self.op(eng, lambda e: e.memset(out, val), writes=[out])

    def recip(self, out, in_):
        self.op('dve', lambda e: e.reciprocal(out=out, in_=in_), reads=[in_], writes=[out])

    def barrier(self):
        evs = []
        for q, ring in self.rings.items():
            for k in range(ring['n']):
                if ring['uses'][k] > 0:
                    evs.append(((q, k), 16 * ring['uses'][k]))
        for e in self.ENGS[:4]:
            if self.cnt[e] > 0:
                evs.append((e, self.cnt[e]))
        for eng in self.ENGS:
            waits = []
            for sk, v in evs:
                if self.waited[eng].get(sk, 0) < v:
                    self.waited[eng][sk] = v
                    waits.append((sk, v))
            if waits:
                self.ops[eng].append((waits, None, None, 0))
    for f in (mm, tr, act, tt, stt, ts, copy, memset, recip, barrier):
        setattr(Sched, f.__name__, f)


_add_helpers()


D = 2048
DFF = 5632
NCH = 16
NF = 44
TB = 512
EPS = 1e-6


class RowCtx:
    def __init__(self, S):
        self.S = S
        self.ones = S.sb([128, 128], F32, 'ones')
        self.eps = S.sb([128, 1], F32, 'epsc')
        self.xt = S.sb([128, NCH, TB], F32, 'xt')
        self.hT = S.sb([128, NCH, TB], BF16, 'hT')
        self.aT = S.sb([128, NF, TB], BF16, 'aT')
        self.sq = [S.sb([128, TB], F32, 'sq%d' % i) for i in range(2)]
        self.sg = [S.sb([128, TB], F32, 'sg%d' % i) for i in range(2)]
        self.rstd = S.sb([128, TB], F32, 'rstd')
        self.wbuf = [S.sb([128, 16384], BF16, 'wbuf%d' % i) for i in range(2)]
        self.pb = [S.ps([128, TB], F32, 'pb%d' % i) for i in range(8)]
        S.op('dve', lambda e: e.memset(self.ones[:], 1.0), writes=[self.ones])
        S.op('dve', lambda e: e.memset(self.eps[:], EPS), writes=[self.eps])


def rmsnorm(S, C, xt, gcol, hT, ssp, w=TB):
    for c in range(NCH):
        sq = C.sq[c % 2]
        S.op('act', lambda e, c=c, sq=sq: e.activation(out=sq[:, 0:w], in_=xt[:, c, :], func=AF.Square),
             reads=[xt], writes=[sq])
        S.op('pe', lambda e, c=c, sq=sq: e.matmul(ssp[:, 0:w], lhsT=C.ones[:], rhs=sq[:, 0:w], start=(c == 0), stop=(c == NCH - 1)),
             reads=[sq, C.ones], writes=[ssp])
    S.op('act', lambda e: e.activation(out=C.rstd[:, 0:w], in_=ssp[:, 0:w], func=AF.Sqrt, scale=1.0 / D, bias=C.eps[:, 0:1]),
         reads=[ssp, C.eps], writes=[C.rstd])
    S.op('dve', lambda e: e.reciprocal(out=C.rstd[:, 0:w], in_=C.rstd[:, 0:w]), reads=[C.rstd], writes=[C.rstd])
    for c in range(NCH):
        S.op('dve', lambda e, c=c: e.scalar_tensor_tensor(out=hT[:, c, :], in0=xt[:, c, :], scalar=gcol[:, c:c + 1],
                                                          in1=C.rstd[:, 0:w], op0=ALU.mult, op1=ALU.mult),
             reads=[xt, gcol, C.rstd], writes=[hT])


def ffn_tiles(S, C, w_gu, w_down):
    steps = []
    wgu_v = w_gu.rearrange("(c p) n -> p c n", p=128)
    wd_v = w_down.rearrange("(f p) n -> p f n", p=128)
    for fg in range(NF // 4):
        def load(buf, fg=fg):
            gv = buf[:, 0:8192].rearrange("p (c n) -> p c n", n=512)
            uv = buf[:, 8192:16384].rearrange("p (c n) -> p c n", n=512)
            S.dma('pool', gv, wgu_v[:, :, fg * 512:(fg + 1) * 512], reads=[], writes=[buf])
            S.dma('pool', uv, wgu_v[:, :, DFF + fg * 512:DFF + (fg + 1) * 512], reads=[], writes=[buf])

        def compute(buf, fg=fg):
            gv = buf[:, 0:8192].rearrange("p (c n) -> p c n", n=512)
            uv = buf[:, 8192:16384].rearrange("p (c n) -> p c n", n=512)
            for j in range(4):
                f = fg * 4 + j
                pg = C.pb[(f % 2) * 2]
                pu = C.pb[(f % 2) * 2 + 1]
                sg = C.sg[f % 2]
                for c in range(NCH):
                    S.op('pe', lambda e, c=c, j=j, pg=pg: e.matmul(pg[:], lhsT=gv[:, c, j * 128:(j + 1) * 128], rhs=C.hT[:, c, :],
                                                                  start=(c == 0), stop=(c == NCH - 1)),
                         reads=[buf, C.hT], writes=[pg])
                for c in range(NCH):
                    S.op('pe', lambda e, c=c, j=j, pu=pu: e.matmul(pu[:], lhsT=uv[:, c, j * 128:(j + 1) * 128], rhs=C.hT[:, c, :],
                                                                  start=(c == 0), stop=(c == NCH - 1)),
                         reads=[buf, C.hT], writes=[pu])
                S.op('act', lambda e, pg=pg, sg=sg: e.activation(out=sg[:], in_=pg[:], func=AF.Silu), reads=[pg], writes=[sg])
                S.op('dve', lambda e, f=f, sg=sg, pu=pu: e.tensor_tensor(out=C.aT[:, f, :], in0=sg[:], in1=pu[:], op=ALU.mult),
                     reads=[sg, pu], writes=[C.aT])
        steps.append((load, compute))
    for g in range(NCH // 2):
        def load(buf, g=g):
            wv = buf[:, 0:NF * 256].rearrange("p (f n) -> p f n", n=256)
            S.dma('pool', wv[:, 0:22, :], wd_v[:, 0:22, g * 256:(g + 1) * 256], reads=[], writes=[buf])
            S.dma('pool', wv[:, 22:44, :], wd_v[:, 22:44, g * 256:(g + 1) * 256], reads=[], writes=[buf])

        def compute(buf, g=g):
            wv = buf[:, 0:NF * 256].rearrange("p (f n) -> p f n", n=256)
            for j in range(2):
                dm = g * 2 + j
                pd = C.pb[4 + dm % 2]
                for f in range(NF):
                    S.op('pe', lambda e, f=f, j=j, pd=pd: e.matmul(pd[:], lhsT=wv[:, f, j * 128:(j + 1) * 128], rhs=C.aT[:, f, :],
                                                                  start=(f == 0), stop=(f == NF - 1)),
                         reads=[buf, C.aT], writes=[pd])
                S.op('dve', lambda e, dm=dm, pd=pd: e.scalar_tensor_tensor(out=C.xt[:, dm, :], in0=pd[:], scalar=0.5, in1=C.xt[:, dm, :],
                                                                          op0=ALU.mult, op1=ALU.add),
                     reads=[pd, C.xt], writes=[C.xt])
        steps.append((load, compute))
    return steps


def run_steps(S, C, steps):
    n = len(steps)
    steps[0][0](C.wbuf[0])
    for i in range(n):
        if i + 1 < n:
            steps[i + 1][0](C.wbuf[(i + 1) % 2])
        steps[i][1](C.wbuf[i % 2])


def build_ffn_only(T):
    nc = bass.Bass("TRN2", target_bir_lowering=False)
    xT = nc.dram_tensor("xT", [D, T], F32, kind="ExternalInput").ap()
    yT = nc.dram_tensor("yT", [D, T], F32, kind="ExternalOutput").ap()
    g = nc.dram_tensor("g", [128, NCH], F32, kind="ExternalInput").ap()
    w_gu = nc.dram_tensor("w_gu", [D, 2 * DFF], F32, kind="ExternalInput").ap()
    w_down = nc.dram_tensor("w_down", [DFF, D], F32, kind="ExternalInput").ap()
    with ExitStack() as es:
        S = Sched(nc, es)
        C = RowCtx(S)
        gcol = S.sb([128, NCH], F32, 'gcol')
        S.dma('sp', gcol[:], g)
        xv = xT.rearrange("(c p) t -> p c t", p=128)
        yv = yT.rearrange("(c p) t -> p c t", p=128)
        for b in range(T // TB):
            S.dma('sp', C.xt[:], xv[:, :, b * TB:(b + 1) * TB])
            rmsnorm(S, C, C.xt, gcol, C.hT, C.pb[6])
            run_steps(S, C, ffn_tiles(S, C, w_gu, w_down))
            S.dma('sp', yv[:, :, b * TB:(b + 1) * TB], C.xt[:])
        S.finish()
        S.emit()
    return nc


HDX = 128


def outproj_tiles(S, C, w_out):
    steps = []
    wv_ = w_out.rearrange("(c p) n -> p c n", p=128)
    for half in range(2):
        def load(buf, half=half):
            v = buf[:].rearrange("p (c n) -> p c n", n=1024)
            S.dma('pool', v[:, 0:8, :], wv_[:, 0:8, half * 1024:(half + 1) * 1024], reads=[], writes=[buf])
            S.dma('pool', v[:, 8:16, :], wv_[:, 8:16, half * 1024:(half + 1) * 1024], reads=[], writes=[buf])

        def compute(buf, half=half):
            v = buf[:].rearrange("p (c n) -> p c n", n=1024)
            for j in range(8):
                dm = half * 8 + j
                pd = C.pb[4 + dm % 2]
                for c in range(NCH):
                    S.mm(pd[:], v[:, c, j * 128:(j + 1) * 128], C.hT[:, c, :], start=(c == 0), stop=(c == NCH - 1))
                S.tt('dve', C.xt[:, dm, :], C.xt[:, dm, :], pd[:], ALU.add)
        steps.append((load, compute))
    return steps


def xattn_setup(S, C, X, memT, g_mem, w_kv, xg):
    mv = C.xt[:, :, 0:256]
    S.dma('sp', mv, memT.rearrange("(c p) t -> p c t", p=128))
    mn = C.aT[:, 0:8, :].rearrange("p a (b t) -> p (a b) t", b=2)
    rmsnorm(S, C, mv, g_mem, mn, C.pb[6], w=256)
    buf = C.wbuf[0]
    wk = buf[:].rearrange("p (c n) -> p c n", n=1024)
    wkv_v = w_kv.rearrange("(c p) n -> p c n", p=128)
    S.dma('pool', wk[:, 0:8, :], wkv_v[:, 0:8, :], reads=[], writes=[buf])
    S.dma('pool', wk[:, 8:16, :], wkv_v[:, 8:16, :], reads=[], writes=[buf])
    for h in range(4):
        pk = C.pb[h % 2]
        for c in range(NCH):
            S.mm(pk[:, 0:256], wk[:, c, h * 128:(h + 1) * 128], mn[:, c, :], start=(c == 0), stop=(c == NCH - 1))
        S.act(C.sq[0][:, 0:256], pk[:, 0:256], AF.Square)
        S.mm(C.pb[6][:, 0:256], C.ones[:], C.sq[0][:, 0:256])
        S.act(C.rstd[:, 0:256], C.pb[6][:, 0:256], AF.Sqrt, scale=1.0 / HDX, bias=C.eps[:, 0:1])
        S.recip(C.rstd[:, 0:256], C.rstd[:, 0:256])
        S.stt('dve', X['kx'][:, h, :], pk[:, 0:256], xg[:, 1:2], C.rstd[:, 0:256], ALU.mult, ALU.mult)
    for mt in range(2):
        pv = C.pb[2 + mt]
        for c in range(NCH):
            S.mm(pv[:], mn[:, c, mt * 128:(mt + 1) * 128], wk[:, c, 512:1024], start=(c == 0), stop=(c == NCH - 1))
        S.copy('act', X['vx'][:, mt, :], pv[:])


def xattn_tiles(S, C, X, w_q, w_o, xg):
    def load(buf):
        wq = buf[:, 0:8192].rearrange("p (c n) -> p c n", n=512)
        wo = buf[:, 8192:16384].rearrange("p (h n) -> p h n", n=2048)
        S.dma('pool', wq, w_q.rearrange("(c p) n -> p c n", p=128), reads=[], writes=[buf])
        S.dma('pool', wo, w_o.rearrange("(h p) n -> p h n", p=128), reads=[], writes=[buf])

    def compute(buf):
        wq = buf[:, 0:8192].rearrange("p (c n) -> p c n", n=512)
        wo = buf[:, 8192:16384].rearrange("p (h n) -> p h n", n=2048)
        scale = float(HDX ** -0.5)
        for h in range(4):
            pq = C.pb[0]
            for c in range(NCH):
                S.mm(pq[:], wq[:, c, h * 128:(h + 1) * 128], C.hT[:, c, :], start=(c == 0), stop=(c == NCH - 1))
            S.act(C.sq[0][:], pq[:], AF.Square)
            S.mm(C.pb[6][:], C.ones[:], C.sq[0][:])
            S.act(C.rstd[:], C.pb[6][:], AF.Sqrt, scale=1.0 / HDX, bias=C.eps[:, 0:1])
            S.recip(C.rstd[:], C.rstd[:])
            S.stt('dve', X['qx'][:], pq[:], xg[:, 0:1], C.rstd[:], ALU.mult, ALU.mult)
            for mt in range(2):
                ps_ = C.pb[1]
                S.mm(ps_[:], X['kx'][:, h, mt * 128:(mt + 1) * 128], X['qx'][:])
                S.act(X['P'][mt][:], ps_[:], AF.Exp, scale=scale)
            for mt in range(2):
                S.mm(C.pb[2][:], X['vx'][:, mt, h * 128:(h + 1) * 128], X['P'][mt][:], start=(mt == 0), stop=(mt == 1))
            for mt in range(2):
                S.mm(C.pb[3][:], X['onesb'][:], X['P'][mt][:], start=(mt == 0), stop=(mt == 1))
            S.recip(C.sg[0][:], C.pb[3][:])
            S.tt('dve', X['ox'][:, h, :], C.sg[0][:], C.pb[2][:], ALU.mult)
        for dm in range(NCH):
            pd = C.pb[4 + dm % 2]
            for h in range(4):
                S.mm(pd[:], wo[:, h, dm * 128:(dm + 1) * 128], X['ox'][:, h, :], start=(h == 0), stop=(h == 3))
            S.tt('dve', C.xt[:, dm, :], C.xt[:, dm, :], pd[:], ALU.add)
    return [(load, compute)]


def build_row(T, mode):
    nc = bass.Bass("TRN2", target_bir_lowering=False)
    xT = nc.dram_tensor("xT", [D, T], F32, kind="ExternalInput").ap()
    yT = nc.dram_tensor("yT", [D, T], F32, kind="ExternalOutput").ap()
    gains = nc.dram_tensor("gains", [128, 4, NCH], F32, kind="ExternalInput").ap()
    nffn = {'first': 1, 'mid': 2, 'last': 1}[mode]
    wgu = [nc.dram_tensor("w_gu%d" % i, [D, 2 * DFF], F32, kind="ExternalInput").ap() for i in range(nffn)]
    wdn = [nc.dram_tensor("w_down%d" % i, [DFF, D], F32, kind="ExternalInput").ap() for i in range(nffn)]
    if mode != 'first':
        mixT = nc.dram_tensor("mixT", [D, T], F32, kind="ExternalInput").ap()
        memT = nc.dram_tensor("memT", [D, 256], F32, kind="ExternalInput").ap()
        w_out = nc.dram_tensor("w_out", [D, D], F32, kind="ExternalInput").ap()
        w_q = nc.dram_tensor("w_q", [D, 512], F32, kind="ExternalInput").ap()
        w_kv = nc.dram_tensor("w_kv", [D, 1024], F32, kind="ExternalInput").ap()
        w_o = nc.dram_tensor("w_o", [512, D], F32, kind="ExternalInput").ap()
        xgain = nc.dram_tensor("xgain", [128, 2], F32, kind="ExternalInput").ap()
    with ExitStack() as es:
        S = Sched(nc, es)
        C = RowCtx(S)
        gn = S.sb([128, 4, NCH], F32, 'gn')
        S.dma('sp', gn[:], gains)
        if mode != 'first':
            xg = S.sb([128, 2], F32, 'xg')
            S.dma('sp', xg[:], xgain)
            X = {'kx': S.sb([128, 4, 256], BF16, 'kx'), 'vx': S.sb([128, 2, 512], BF16, 'vx'), 'qx': S.sb([128, TB], BF16, 'qx'),
                 'P': [S.sb([128, TB], BF16, 'Px%d' % i) for i in range(2)], 'ox': S.sb([128, 4, TB], BF16, 'ox'),
                 'onesb': S.sb([128, 128], BF16, 'onesbx')}
            S.memset('dve', X['onesb'][:], 1.0)
            xattn_setup(S, C, X, memT, gn[:, 1, :], w_kv, xg)
        xv = xT.rearrange("(c p) t -> p c t", p=128)
        yv = yT.rearrange("(c p) t -> p c t", p=128)
        for b in range(T // TB):
            bs = slice(b * TB, (b + 1) * TB)
            S.dma('sp', C.xt[:], xv[:, :, bs])
            if mode == 'first':
                rmsnorm(S, C, C.xt, gn[:, 2, :], C.hT, C.pb[6])
                run_steps(S, C, ffn_tiles(S, C, wgu[0], wdn[0]))
            else:
                S.dma('pool', C.hT[:], mixT.rearrange("(c p) t -> p c t", p=128)[:, :, bs])
                run_steps(S, C, outproj_tiles(S, C, w_out))
                rmsnorm(S, C, C.xt, gn[:, 0, :], C.hT, C.pb[6])
                run_steps(S, C, xattn_tiles(S, C, X, w_q, w_o, xg))
                rmsnorm(S, C, C.xt, gn[:, 2, :], C.hT, C.pb[6])
                run_steps(S, C, ffn_tiles(S, C, wgu[0], wdn[0]))
                if mode == 'mid':
                    rmsnorm(S, C, C.xt, gn[:, 3, :], C.hT, C.pb[6])
                    run_steps(S, C, ffn_tiles(S, C, wgu[1], wdn[1]))
            S.dma('sp', yv[:, :, bs], C.xt[:])
        S.finish()
        S.emit()
    return nc


HD = 128
NCOL = 3592
ROPE_THETA = 500000.0
C_LE, C_GT, C_ID, C_TRIL, C_ROT, C_INVF = 0, 128, 256, 384, 512, 640
NCONST = 641


def make_consts():
    c = np.zeros((128, NCONST), np.float32)
    r = np.arange(128)[:, None]
    q = np.arange(128)[None, :]
    c[:, C_LE:C_LE + 128] = (q >= r)
    c[:, C_GT:C_GT + 128] = (r > q)
    c[:, C_ID:C_ID + 128] = (r == q)
    c[:, C_TRIL:C_TRIL + 128] = (q <= r)
    rot = np.zeros((128, 128), np.float32)
    for p in range(16):
        rot[p + 16, p] = -1.0
        rot[p, p + 16] = 1.0
    c[:, C_ROT:C_ROT + 128] = rot
    invf = np.zeros(128, np.float32)
    half = 16
    fr = np.float32(ROPE_THETA) ** (-(np.arange(half, dtype=np.float32) / np.float32(half)))
    invf[0:16] = fr
    invf[16:32] = fr
    c[:, C_INVF] = invf
    return c


def head_rms_rope(S, K, src, dst, gain_col, posf, n, do_scale=None):
    for s0 in range(0, n, 512):
        w = min(512, n - s0)
        sl = slice(s0, s0 + w)
        sq = K['sq'][(s0 // 512) % 2]
        S.act(sq[:, 0:w], src[:, sl], AF.Square)
        ssp = K['pss']
        S.mm(ssp[:, 0:w], K['ones'][:], sq[:, 0:w])
        rstd = K['rstd']
        S.act(rstd[:, 0:w], ssp[:, 0:w], AF.Sqrt, scale=1.0 / HD, bias=K['eps'][:, 0:1])
        S.recip(rstd[:, 0:w], rstd[:, 0:w])
        qn = K['qn']
        S.stt('dve', qn[:, 0:w], src[:, sl], gain_col, rstd[:, 0:w], ALU.mult, ALU.mult)
        ang = K['ang']
        S.ts('dve', ang[:, 0:w], posf[:, sl], K['invf'], None, ALU.mult)
        sn = K['sn']
        cs = K['cs']
        for (dstt, shift) in ((sn, 0.0), (cs, float(0.5 * np.pi))):
            a2 = K['a2']
            S.ts('dve', a2[:, 0:w], ang[:, 0:w], shift, None, ALU.add)
            S.ts('dve', K['kf'][:, 0:w], a2[:, 0:w], float(1.0 / (2 * np.pi)), None, ALU.mult)
            S.copy('dve', K['ti'][:, 0:w], K['kf'][:, 0:w])
            S.copy('dve', K['kf'][:, 0:w], K['ti'][:, 0:w])
            S.stt('dve', a2[:, 0:w], K['kf'][:, 0:w], float(-2 * np.pi), a2[:, 0:w], ALU.mult, ALU.add)
            S.ts('dve', a2[:, 0:w], a2[:, 0:w], float(-np.pi), float(np.pi), ALU.max, ALU.min)
            S.act(dstt[:, 0:w], a2[:, 0:w], AF.Sin)
        rp = K['prot']
        S.mm(rp[:, 0:w], K['rot'], qn[:, 0:w])
        S.tt('dve', sn[:, 0:w], sn[:, 0:w], rp[:, 0:w], ALU.mult)
        S.tt('dve', cs[:, 0:w], cs[:, 0:w], qn[:, 0:w], ALU.mult)
        S.tt('dve', dst[:, sl], cs[:, 0:w], sn[:, 0:w], ALU.add)


def build_even(SEQ, phases='A12', dbg=9):
    NT = SEQ // 128
    NB = SEQ // TB
    nc = bass.Bass("TRN2", target_bir_lowering=False)
    xT = nc.dram_tensor("xT", [D, SEQ], F32, kind="ExternalInput").ap()
    gmix = nc.dram_tensor("gmix", [128, NCH], F32, kind="ExternalInput").ap()
    w_in = nc.dram_tensor("w_in", [D, NCOL], F32, kind="ExternalInput").ap()
    convw = nc.dram_tensor("convw", [128, 12, 4], F32, kind="ExternalInput").ap()
    gvec = nc.dram_tensor("gvec", [128, 16], F32, kind="ExternalInput").ap()
    pos = nc.dram_tensor("pos", [128, SEQ], I32, kind="ExternalInput").ap()
    consts = nc.dram_tensor("consts", [128, NCONST], F32, kind="ExternalInput").ap()
    mixT = nc.dram_tensor("mixT", [1024, SEQ], F32, kind="ExternalOutput").ap()
    projT = nc.dram_tensor("projT", [28 * 128, SEQ], F32, kind="Internal").ap()

    with ExitStack() as es:
        S = Sched(nc, es)
        cst = S.sb([128, NCONST], F32, 'cst')
        S.dma('sp', cst[:], consts)
        gv = S.sb([128, 16], F32, 'gv')
        S.dma('sp', gv[:], gvec)
        cw = S.sb([128, 12, 4], F32, 'cw')
        S.dma('sp', cw[:], convw)
        LE = cst[:, C_LE:C_LE + 128]
        GT = cst[:, C_GT:C_GT + 128]
        ID = cst[:, C_ID:C_ID + 128]
        G = {k: S.sb([128, NT, 4], F32, 'G_' + k) for k in ('raw_b', 'raw_a', 'beta', 'g', 'gam', 'eg', 'bg', 'ekd', 'cd')}
        negA = S.sb([128, 4], F32, 'negA')
        S.act(negA[:], gv[:, 0:4], AF.Exp)
        S.ts('dve', negA[:], negA[:], -1.0, None, ALU.mult)

        with ExitStack() as es2:
            S.es = es2
            C = RowCtx.__new__(RowCtx)
            C.S = S
            C.ones = S.sb([128, 128], F32, 'ones')
            C.eps = S.sb([128, 1], F32, 'epsc')
            C.xt = S.sb([128, NCH, TB], F32, 'xt')
            C.hT = S.sb([128, NCH, TB], BF16, 'hT')
            C.sq = [S.sb([128, TB], F32, 'sq%d' % i) for i in range(2)]
            C.rstd = S.sb([128, TB], F32, 'rstd')
            S.memset('dve', C.ones[:], 1.0)
            S.memset('dve', C.eps[:], EPS)
            gcol = S.sb([128, NCH], F32, 'gcol')
            S.dma('sp', gcol[:], gmix)
            W = S.sb([128, NCH, NCOL], BF16, 'Win')
            wv = w_in.rearrange("(c p) n -> p c n", p=128)
            for i in range(7):
                S.dma('pool', W[:, :, i * 512:(i + 1) * 512], wv[:, :, i * 512:(i + 1) * 512])
            S.dma('pool', W[:, :, 3584:NCOL], wv[:, :, 3584:NCOL])
            pb = [S.ps([128, TB], F32, 'pa%d' % i) for i in range(4)]
            pss = S.ps([128, TB], F32, 'pass')
            pg = S.ps([128, 8], F32, 'pg')
            stg = [S.sb([128, TB], F32, 'stg%d' % i) for i in range(4)]
            xv = xT.rearrange("(c p) t -> p c t", p=128)
            for b in range(NB if dbg >= 2 else 0):
                S.dma('sp', C.xt[:], xv[:, :, b * TB:(b + 1) * TB])
                rmsnorm(S, C, C.xt, gcol, C.hT, pss)
                for ch in range(28):
                    p = pb[ch % 4]
                    for c in range(NCH):
                        S.mm(p[:], W[:, c, ch * 128:(ch + 1) * 128], C.hT[:, c, :], start=(c == 0), stop=(c == NCH - 1))
                    st = stg[ch % 4]
                    S.copy('act' if ch % 2 == 0 else 'dve', st[:], p[:])
                    S.dma('sp', projT[ch * 128:(ch + 1) * 128, b * TB:(b + 1) * TB], st[:], writes=[('proj', ch)])
                for tt in range(4 if dbg >= 3 else 0):
                    t = b * 4 + tt
                    for c in range(NCH):
                        S.mm(pg[:], C.hT[:, c, tt * 128:(tt + 1) * 128], W[:, c, 3584:NCOL], start=(c == 0), stop=(c == NCH - 1))
                    S.copy('dve', G['raw_b'][:, t, :], pg[:, 0:4])
                    S.copy('dve', G['raw_a'][:, t, :], pg[:, 4:8])
            tmp = S.sb([128, NT, 4], F32, 'gtmp')
            for h in range(4 if dbg >= 4 else 0):
                S.act(G['beta'][:, :, h], G['raw_b'][:, :, h], AF.Sigmoid)
            for h in range(4 if dbg >= 4 else 0):
                S.ts('dve', tmp[:, :, h], G['raw_a'][:, :, h], gv[:, 4 + h:5 + h], None, ALU.add)
            if dbg >= 5:
                S.act(tmp[:], tmp[:], AF.Exp)
                S.act(tmp[:], tmp[:], AF.Ln, bias=C.ones[:, 0:1])
            for h in range(4 if dbg >= 5 else 0):
                S.ts('dve', G['g'][:, :, h], tmp[:, :, h], negA[:, h:h + 1], None, ALU.mult)
            gflat = G['g'][:].rearrange("p t h -> p (t h)")
            for c0 in range(0, NT * 4 if dbg >= 6 else 0, 512):
                w = min(512, NT * 4 - c0)
                sl = slice(c0, c0 + w)
                S.mm(pb[0][:, 0:w], LE, gflat[:, sl])
                S.mm(pb[1][:, 0:w], C.ones[:], gflat[:, sl])
                S.copy('act', G['gam'][:].rearrange("p t h -> p (t h)")[:, sl], pb[0][:, 0:w])
                S.act(G['eg'][:].rearrange("p t h -> p (t h)")[:, sl], pb[0][:, 0:w], AF.Exp)
                S.act(G['cd'][:].rearrange("p t h -> p (t h)")[:, sl], pb[1][:, 0:w], AF.Exp)
                S.tt('dve', G['ekd'][:].rearrange("p t h -> p (t h)")[:, sl], pb[1][:, 0:w],
                     G['gam'][:].rearrange("p t h -> p (t h)")[:, sl], ALU.subtract)
            if dbg >= 7:
                S.act(G['ekd'][:], G['ekd'][:], AF.Exp)
                S.tt('dve', G['bg'][:], G['beta'][:], G['eg'][:], ALU.mult)
            S.barrier()
        with ExitStack() as es2:
          if '1' in phases:
            S.es = es2
            ones = S.sb([128, 128], F32, 'ones_b')
            eps = S.sb([128, 1], F32, 'eps_b')
            S.memset('dve', ones[:], 1.0)
            S.memset('dve', eps[:], EPS)
            raw = S.sb([128, SEQ + 3], F32, 'raw')
            S.memset('dve', raw[:, 0:3], 0.0)
            qT = S.sb([128, SEQ], F32, 'qT')
            kT = S.sb([128, SEQ], F32, 'kT')
            vT = S.sb([128, SEQ], F32, 'vT')
            oT = S.sb([128, SEQ], F32, 'oT')
            sq = [S.sb([128, 512], F32, 'sqb%d' % i) for i in range(2)]
            rstd = S.sb([128, 512], F32, 'rstdb')
            pss = S.ps([128, 512], F32, 'pssb')
            NPS = 7
            pbk = [S.ps([128, 512], F32, 'pbk%d' % i) for i in range(NPS)]
            pst = [pbk[i][:, 0:128] for i in range(NPS)]
            psi = [0]

            def nps():
                psi[0] += 1
                return pst[psi[0] % NPS]
            names = ['gm1', 'gm2', 'DT', 'Dl', 'M0', 'U0', 'M1', 'U1', 'P', 'qk', 'kbg', 'kdec', 'vb', 'u', 'wT', 'dg', 'qd', 'vn']
            CT = [{k: S.sb([128, 128], F32, '%s_%d' % (k, i)) for k in names} for i in range(2)]
            Sst = S.sb([128, 128], F32, 'Sst')
            for hl in range(4):
                for (ch, dst, cwi) in ((hl, qT, hl), (4 + hl, kT, 4 + hl), (8 + hl, vT, 8 + hl)):
                    S.dma('sp', raw[:, 3:3 + SEQ], projT[ch * 128:(ch + 1) * 128, :], reads=[('proj', ch)])
                    S.ts('dve', dst[:], raw[:, 3:3 + SEQ], cw[:, cwi, 3:4], None, ALU.mult)
                    for k in (2, 1, 0):
                        S.stt('dve', dst[:], raw[:, k:k + SEQ], cw[:, cwi, k:k + 1], dst[:], ALU.mult, ALU.add)
                    S.act(dst[:], dst[:], AF.Silu)
                for (t_, scl) in ((qT, HD ** -0.5), (kT, 1.0)):
                    for s0 in range(0, SEQ, 512):
                        sl = slice(s0, s0 + 512)
                        sqb = sq[(s0 // 512) % 2]
                        S.act(sqb[:], t_[:, sl], AF.Square)
                        S.mm(pss[:], ones[:], sqb[:])
                        S.act(rstd[:], pss[:], AF.Sqrt, bias=eps[:, 0:1])
                        S.recip(rstd[:], rstd[:])
                        S.stt('dve', t_[:, sl], t_[:, sl], float(scl), rstd[:], ALU.mult, ALU.mult)
                S.memset('dve', Sst[:], 0.0)
                for n in range(NT):
                    T = CT[n % 2]
                    cs = slice(n * 128, (n + 1) * 128)

                    def col(k):
                        return G[k][:, n, hl:hl + 1]
                    S.ts('dve', T['gm1'][:], GT, col('g'), None, ALU.mult)
                    S.ts('dve', T['gm2'][:], LE, col('g'), None, ALU.mult)
                    p1 = nps()
                    S.mm(p1, T['gm1'][:], LE)
                    p2 = nps()
                    S.mm(p2, T['gm2'][:], GT)
                    S.act(T['DT'][:], p1, AF.Exp)
                    S.act(T['Dl'][:], p2, AF.Exp)
                    S.tt('pool', T['DT'][:], T['DT'][:], LE, ALU.mult)
                    S.tt('pool', T['Dl'][:], T['Dl'][:], GT, ALU.mult)
                    pkk = nps()
                    S.mm(pkk, kT[:, cs], kT[:, cs])
                    pkq = nps()
                    S.mm(pkq, kT[:, cs], qT[:, cs])
                    S.stt('dve', T['M0'][:], pkk, col('beta'), T['Dl'][:], ALU.mult, ALU.mult)
                    S.tt('dve', T['qk'][:], pkq, T['DT'][:], ALU.mult)
                    pu = nps()
                    S.tr(pu, T['M0'][:], ID)
                    S.copy('act', T['U0'][:], pu)
                    S.stt('dve', T['P'][:], pu, -1.0, ID, ALU.mult, ALU.add)
                    Mc, Uc, Mn, Un = T['M0'], T['U0'], T['M1'], T['U1']
                    for k in range(1, 7):
                        pm = nps()
                        S.mm(pm, Uc[:], Mc[:])
                        S.copy('act', Mn[:], pm)
                        if k < 6:
                            pu2 = nps()
                            S.mm(pu2, Mc[:], Uc[:])
                            S.copy('act', Un[:], pu2)
                        pp = nps()
                        S.mm(pp, Mn[:], T['P'][:])
                        S.tt('dve', T['P'][:], T['P'][:], pp, ALU.add)
                        Mc, Uc, Mn, Un = Mn, Un, Mc, Uc
                    pk = nps()
                    S.tr(pk, kT[:, cs], ID)
                    S.ts('dve', T['kbg'][:], pk, col('bg'), None, ALU.mult)
                    S.op('act', lambda e, o=T['kdec'][:], i=pk, m=col('ekd'): e.mul(out=o, in_=i, mul=m),
                         reads=[pk, G['ekd']], writes=[T['kdec']])
                    pv = nps()
                    S.tr(pv, vT[:, cs], ID)
                    S.ts('dve', T['vb'][:], pv, col('beta'), None, ALU.mult)
                    pu_ = nps()
                    S.mm(pu_, T['P'][:], T['vb'][:])
                    S.copy('act', T['u'][:], pu_)
                    pw = nps()
                    S.mm(pw, T['kbg'][:], T['P'][:])
                    S.copy('act', T['wT'][:], pw)
                    S.ts('pool', T['dg'][:], ID, col('eg'), None, ALU.mult)
                    pe_ = nps()
                    S.mm(pe_, ones[:], T['dg'][:])
                    S.tt('dve', T['qd'][:], qT[:, cs], pe_, ALU.mult)
                    pws = nps()
                    S.mm(pws, T['wT'][:], Sst[:])
                    S.tt('dve', T['vn'][:], T['u'][:], pws, ALU.subtract)
                    po = nps()
                    S.mm(po, Sst[:], T['qd'][:], start=True, stop=False)
                    S.mm(po, T['vn'][:], T['qk'][:], start=False, stop=True)
                    S.copy('act', oT[:, cs], po)
                    psn = nps()
                    S.mm(psn, T['kdec'][:], T['vn'][:])
                    S.stt('dve', Sst[:], Sst[:], col('cd'), psn, ALU.mult, ALU.add)
                ch = 12 + hl
                S.dma('sp', raw[:, 3:3 + SEQ], projT[ch * 128:(ch + 1) * 128, :], reads=[('proj', ch)])
                S.act(raw[:, 3:3 + SEQ], raw[:, 3:3 + SEQ], AF.Silu)
                for s0 in range(0, SEQ, 512):
                    sl = slice(s0, s0 + 512)
                    sqb = sq[(s0 // 512) % 2]
                    S.act(sqb[:], oT[:, sl], AF.Square)
                    S.mm(pss[:], ones[:], sqb[:])
                    S.act(rstd[:], pss[:], AF.Sqrt, scale=1.0 / HD, bias=eps[:, 0:1])
                    S.recip(rstd[:], rstd[:])
                    S.stt('dve', oT[:, sl], oT[:, sl], gv[:, 8:9], rstd[:], ALU.mult, ALU.mult)
                S.tt('dve', oT[:], oT[:], raw[:, 3:3 + SEQ], ALU.mult)
                S.dma('sp', mixT[hl * 128:(hl + 1) * 128, :], oT[:])
            S.barrier()
        with ExitStack() as es2:
          if '2' in phases:
            S.es = es2
            K = {}
            K['ones'] = S.sb([128, 128], F32, 'ones_c')
            K['eps'] = S.sb([128, 1], F32, 'eps_c')
            K['negpi'] = S.sb([128, 1], F32, 'negpi')
            S.memset('dve', K['ones'][:], 1.0)
            S.memset('dve', K['eps'][:], EPS)
            S.memset('dve', K['negpi'][:], -float(np.pi))
            K['sq'] = [S.sb([128, 512], F32, 'sqc%d' % i) for i in range(2)]
            for k in ('rstd', 'qn', 'ang', 'sn', 'cs', 'a2', 'kf'):
                K[k] = S.sb([128, 512], F32, k + '_c')
            K['ti'] = S.sb([128, 512], I32, 'ti_c')
            K['pss'] = S.ps([128, 512], F32, 'pssc')
            K['prot'] = S.ps([128, 512], F32, 'protc')
            K['rot'] = cst[:, C_ROT:C_ROT + 128]
            K['invf'] = cst[:, C_INVF:C_INVF + 1]
            onesb = S.sb([128, 128], BF16, 'onesb')
            S.memset('dve', onesb[:], 1.0)
            idb = S.sb([128, 128], BF16, 'idb')
            S.copy('dve', idb[:], ID)
            mask2 = S.sb([128, 256], BF16, 'mask2')
            S.copy('dve', mask2[:, 0:128], cst[:, C_TRIL:C_TRIL + 128])
            S.copy('dve', mask2[:, 128:256], LE)
            posi = S.sb([128, SEQ], I32, 'posi')
            S.dma('sp', posi[:], pos)
            posf = posi[:].bitcast(F32)
            S.copy('dve', posf, posi[:])
            raw = S.sb([128, SEQ], F32, 'rawc')
            num = S.sb([128, SEQ], F32, 'num')
            qb = S.sb([128, SEQ], BF16, 'qb')
            kb = S.sb([128, SEQ], BF16, 'kb')
            vb = S.sb([128, SEQ], BF16, 'vb')
            vt = [S.sb([128, 128], BF16, 'vt%d' % i) for i in range(4)]
            Pt = [S.sb([128, 256], BF16, 'Pt%d' % i) for i in range(2)]
            bkS = [S.ps([128, 512], F32, 'bkS%d' % i) for i in range(2)]
            psS = [bkS[i][:, 0:256] for i in range(2)]
            bkO = [S.ps([128, 512], F32, 'bkO%d' % i) for i in range(2)]
            psO = [bkO[i][:, 0:128] for i in range(2)]
            psL = [bkO[i][:, 128:256] for i in range(2)]
            psT = [S.ps([128, 128], BF16, 'psT0')] * 2
            scale = float(HD ** -0.5)
            for hl in range(4):
                S.dma('sp', raw[:], projT[(16 + hl) * 128:(17 + hl) * 128, :], reads=[('proj', 16 + hl)])
                head_rms_rope(S, K, raw, qb, gv[:, 9:10], posf, SEQ)
                S.dma('sp', raw[:], projT[(20 + hl) * 128:(21 + hl) * 128, :], reads=[('proj', 20 + hl)])
                head_rms_rope(S, K, raw, kb, gv[:, 10:11], posf, SEQ)
                S.dma('sp', raw[:], projT[(24 + hl) * 128:(25 + hl) * 128, :], reads=[('proj', 24 + hl)])
                S.copy('act', vb[:], raw[:])
                den = raw
                it = 0
                vti = 0
                for bi, d in enumerate((1, 4, 16)):
                    Sd = SEQ // d
                    ntile = Sd // 128
                    qr = qb[:].rearrange("p (m d) -> p d m", d=d)
                    kr = kb[:].rearrange("p (m d) -> p d m", d=d)
                    vr = vb[:].rearrange("p (m d) -> p d m", d=d)
                    numr = num[:].rearrange("p (m d) -> p d m", d=d)
                    denr = den[:].rearrange("p (m d) -> p d m", d=d)
                    for r in range(d):
                        vcache = {}
                        for qt in range(ntile):
                            kts = [qt - 1, qt] if qt > 0 else [qt]
                            for kt in kts:
                                if kt not in vcache:
                                    pT = psT[vti % 2]
                                    S.tr(pT[:], vr[:, r, kt * 128:(kt + 1) * 128], idb[:])
                                    vtile = vt[vti % 4]
                                    S.copy('act', vtile[:], pT[:])
                                    vcache[kt] = vtile
                                    vti += 1
                            sp_ = psS[it % 2]
                            P = Pt[it % 2]
                            nk = len(kts)
                            for idx, kt in enumerate(kts):
                                S.mm(sp_[:, idx * 128:(idx + 1) * 128], kr[:, r, kt * 128:(kt + 1) * 128], qr[:, r, qt * 128:(qt + 1) * 128])
                            S.act(P[:, 0:nk * 128], sp_[:, 0:nk * 128], AF.Exp, scale=scale)
                            msk = mask2[:, 0:256] if nk == 2 else mask2[:, 128:256]
                            S.tt('pool', P[:, 0:nk * 128], P[:, 0:nk * 128], msk, ALU.mult)
                            po = psO[it % 2]
                            pl = psL[it % 2]
                            for idx, kt in enumerate(kts):
                                S.mm(po, vcache[kt][:], P[:, idx * 128:(idx + 1) * 128], start=(idx == 0), stop=(idx == nk - 1))
                            for idx, kt in enumerate(kts):
                                S.mm(pl, onesb[:], P[:, idx * 128:(idx + 1) * 128], start=(idx == 0), stop=(idx == nk - 1))
                            nsl = numr[:, r, qt * 128:(qt + 1) * 128]
                            dsl = denr[:, r, qt * 128:(qt + 1) * 128]
                            if bi == 0:
                                S.copy('act', nsl, po)
                                S.copy('dve', dsl, pl)
                            else:
                                S.tt('dve', nsl, nsl, po, ALU.add)
                                S.tt('dve', dsl, dsl, pl, ALU.add)
                            it += 1
                S.recip(den[:], den[:])
                S.tt('dve', num[:], num[:], den[:], ALU.mult)
                S.dma('sp', mixT[512 + hl * 128:512 + (hl + 1) * 128, :], num[:])
            S.barrier()
        S.finish()
        S.emit()
    return nc


NCOL_O = 2584
NEGM = 30000.0


def make_consts_odd(SEQ):
    NCMP = (SEQ - 32) // 16 + 1
    NTC = (NCMP + 127) // 128
    out = {}
    n = np.arange(128)[:, None]
    i = np.arange(128)[None, :]
    mc = np.zeros((128, 17, 128), np.float32)
    for j in range(17):
        mc[:, j, :] = ((16 * n + 31 - i) <= 128 * j)
    out['mc'] = mc.reshape(128, 17 * 128)
    ov = np.zeros((128, NTC, 128), np.float32)
    for t in range(NTC):
        nn = t * 128 + np.arange(128)
        cs = nn[:, None] * 16
        ss = np.arange(128)[None, :] * 64
        ov[:, t, :] = ((cs < ss + 64) & (cs + 32 > ss) & (nn[:, None] < NCMP))
    out['ov'] = ov.reshape(128, NTC * 128)
    p = np.arange(128)[:, None]
    c = np.arange(256)[None, :]
    cur = 128 + (p >= 64)
    b0 = np.where(c > cur, -1e30, np.where((c == cur) | (c == cur - 1), 1000.0, 0.0)).astype(np.float32)
    out['b0'] = b0
    ew = (np.arange(SEQ)[None, :] // 64 == np.arange(128)[:, None]).astype(np.float32)
    out['ew'] = ew
    k = np.arange(128)[:, None]
    q = np.arange(128)[None, :]
    m3 = np.zeros((128, 2, 4, 128), np.float32)
    m3[:, 0, :, :] = (q >= k)[:, None, :]
    m3[:, 1, :, :] = (q < k)[:, None, :]
    out['m3'] = m3.reshape(128, 1024)
    sel = np.zeros((24, 24 * 128), np.float32)
    for j in range(24):
        sel[j, j * 128:(j + 1) * 128] = 1.0
    out['sel'] = sel
    return out


def rms_rope_stream(S, K, load_src, dst_fn, gain_col, load_pos, n, cst, post_fn=None):
    for s0 in range(0, n, 512):
        w = min(512, n - s0)
        sl = slice(s0, s0 + w)
        src = load_src(s0, w)
        posi = load_pos(s0, w)
        sq = K['sq'][(s0 // 512) % 2]
        S.act(sq[:, 0:w], src, AF.Square)
        ssp = K['pss']
        S.mm(ssp[:, 0:w], K['ones'][:], sq[:, 0:w])
        rstd = K['rstd']
        S.act(rstd[:, 0:w], ssp[:, 0:w], AF.Sqrt, scale=1.0 / HD, bias=K['eps'][:, 0:1])
        S.recip(rstd[:, 0:w], rstd[:, 0:w])
        qn = K['qn']
        S.stt('dve', qn[:, 0:w], src, gain_col, rstd[:, 0:w], ALU.mult, ALU.mult)
        ang = K['ang']
        S.copy('dve', ang[:, 0:w], posi)
        S.ts('dve', ang[:, 0:w], ang[:, 0:w], K['invf'], None, ALU.mult)
        sn = K['sn']
        cs = K['cs']
        for (dstt, shift) in ((sn, 0.0), (cs, float(0.5 * np.pi))):
            a2 = K['a2']
            S.ts('dve', a2[:, 0:w], ang[:, 0:w], shift, None, ALU.add)
            S.ts('dve', K['kf'][:, 0:w], a2[:, 0:w], float(1.0 / (2 * np.pi)), None, ALU.mult)
            S.copy('dve', K['ti'][:, 0:w], K['kf'][:, 0:w])
            S.copy('dve', K['kf'][:, 0:w], K['ti'][:, 0:w])
            S.stt('dve', a2[:, 0:w], K['kf'][:, 0:w], float(-2 * np.pi), a2[:, 0:w], ALU.mult, ALU.add)
            S.ts('dve', a2[:, 0:w], a2[:, 0:w], float(-np.pi), float(np.pi), ALU.max, ALU.min)
            S.act(dstt[:, 0:w], a2[:, 0:w], AF.Sin)
        rp = K['prot']
        S.mm(rp[:, 0:w], K['rot'], qn[:, 0:w])
        S.tt('dve', sn[:, 0:w], sn[:, 0:w], rp[:, 0:w], ALU.mult)
        S.tt('dve', cs[:, 0:w], cs[:, 0:w], qn[:, 0:w], ALU.mult)
        d_ = dst_fn(s0, w)
        S.tt('dve', d_, cs[:, 0:w], sn[:, 0:w], ALU.add)
        if post_fn is not None:
            post_fn(s0, w, d_)


def build_odd(SEQ, phases='AB'):
    NT = SEQ // 128
    NB = SEQ // TB
    NCMP = (SEQ - 32) // 16 + 1
    NTC = (NCMP + 127) // 128
    nc = bass.Bass("TRN2", target_bir_lowering=False)
    xT = nc.dram_tensor("xT", [D, SEQ], F32, kind="ExternalInput").ap()
    gmix = nc.dram_tensor("gmix", [128, NCH], F32, kind="ExternalInput").ap()
    w_in = nc.dram_tensor("w_in", [D, NCOL_O], F32, kind="ExternalInput").ap()
    gvec = nc.dram_tensor("gvec", [128, 8], F32, kind="ExternalInput").ap()
    cpos = nc.dram_tensor("cpos", [128, 2, 32], F32, kind="ExternalInput").ap()
    cw1 = nc.dram_tensor("cw1", [2, 4096, 256], F32, kind="ExternalInput").ap()
    cw2 = nc.dram_tensor("cw2", [2, 256, 128], F32, kind="ExternalInput").ap()
    pos = nc.dram_tensor("pos", [128, SEQ], I32, kind="ExternalInput").ap()
    consts = nc.dram_tensor("consts", [128, NCONST], F32, kind="ExternalInput").ap()
    c_mc = nc.dram_tensor("c_mc", [128, 17 * 128], F32, kind="ExternalInput").ap()
    c_ov = nc.dram_tensor("c_ov", [128, NTC * 128], F32, kind="ExternalInput").ap()
    c_b0 = nc.dram_tensor("c_b0", [128, 256], F32, kind="ExternalInput").ap()
    c_ew = nc.dram_tensor("c_ew", [128, SEQ], F32, kind="ExternalInput").ap()
    c_m3 = nc.dram_tensor("c_m3", [128, 1024], F32, kind="ExternalInput").ap()
    c_sel = nc.dram_tensor("c_sel", [24, 24 * 128], F32, kind="ExternalInput").ap()
    mixT = nc.dram_tensor("mixT", [1024, SEQ], F32, kind="ExternalOutput").ap()
    projT = nc.dram_tensor("projT", [20 * 128 + 24, SEQ], F32, kind="Internal").ap()
    qsc = nc.dram_tensor("qsc", [8 * 128, SEQ], BF16, kind="Internal").ap()

    with ExitStack() as es:
        S = Sched(nc, es)
        cst = S.sb([128, NCONST], F32, 'cst')
        S.dma('sp', cst[:], consts)
        gv = S.sb([128, 8], F32, 'gv')
        S.dma('sp', gv[:], gvec)
        LE = cst[:, C_LE:C_LE + 128]
        ID = cst[:, C_ID:C_ID + 128]
        with ExitStack() as es2:
          if 'A' in phases:
            S.es = es2
            C = RowCtx.__new__(RowCtx)
            C.S = S
            C.ones = S.sb([128, 128], F32, 'ones')
            C.eps = S.sb([128, 1], F32, 'epsc')
            C.xt = S.sb([128, NCH, TB], F32, 'xt')
            C.hT = S.sb([128, NCH, TB], BF16, 'hT')
            C.sq = [S.sb([128, TB], F32, 'sq%d' % i) for i in range(2)]
            C.rstd = S.sb([128, TB], F32, 'rstd')
            S.memset('dve', C.ones[:], 1.0)
            S.memset('dve', C.eps[:], EPS)
            gcol = S.sb([128, NCH], F32, 'gcol')
            S.dma('sp', gcol[:], gmix)
            W = S.sb([128, NCH, NCOL_O], BF16, 'Win')
            wv = w_in.rearrange("(c p) n -> p c n", p=128)
            for i in range(5):
                S.dma('pool', W[:, :, i * 512:(i + 1) * 512], wv[:, :, i * 512:(i + 1) * 512])
            S.dma('pool', W[:, :, 2560:NCOL_O], wv[:, :, 2560:NCOL_O])
            pb = [S.ps([128, TB], F32, 'pa%d' % i) for i in range(4)]
            pss = S.ps([128, TB], F32, 'pass')
            pg = S.ps([24, TB], F32, 'pg')
            stg = [S.sb([128, TB], F32, 'stg%d' % i) for i in range(4)]
            gst = S.sb([24, TB], F32, 'gst')
            xv = xT.rearrange("(c p) t -> p c t", p=128)
            for b in range(NB):
                S.dma('sp', C.xt[:], xv[:, :, b * TB:(b + 1) * TB])
                rmsnorm(S, C, C.xt, gcol, C.hT, pss)
                for ch in range(20):
                    p = pb[ch % 4]
                    for c in range(NCH):
                        S.mm(p[:], W[:, c, ch * 128:(ch + 1) * 128], C.hT[:, c, :], start=(c == 0), stop=(c == NCH - 1))
                    st = stg[ch % 4]
                    S.copy('act' if ch % 2 == 0 else 'dve', st[:], p[:])
                    S.dma('sp', projT[ch * 128:(ch + 1) * 128, b * TB:(b + 1) * TB], st[:], writes=[('proj', ch)])
                for c in range(NCH):
                    S.mm(pg[:], W[:, c, 2560:NCOL_O], C.hT[:, c, :], start=(c == 0), stop=(c == NCH - 1))
                S.act(gst[:], pg[:], AF.Sigmoid)
                S.dma('sp', projT[2560:2584, b * TB:(b + 1) * TB], gst[:], writes=[('proj', 20)])
            S.barrier()
        with ExitStack() as es2:
          if 'B' in phases:
            S.es = es2
            K = {}
            K['ones'] = S.sb([128, 128], F32, 'ones_c')
            K['eps'] = S.sb([128, 1], F32, 'eps_c')
            S.memset('dve', K['ones'][:], 1.0)
            S.memset('dve', K['eps'][:], EPS)
            K['sq'] = [S.sb([128, 512], F32, 'sqc%d' % i) for i in range(2)]
            for k in ('rstd', 'qn', 'ang', 'sn', 'cs', 'a2', 'kf'):
                K[k] = S.sb([128, 512], F32, k + '_c')
            K['ti'] = S.sb([128, 512], I32, 'ti_c')
            K['pss'] = S.ps([128, 512], F32, 'pssc')
            K['prot'] = S.ps([128, 512], F32, 'protc')
            K['rot'] = cst[:, C_ROT:C_ROT + 128]
            K['invf'] = cst[:, C_INVF:C_INVF + 1]
            pA = S.ps([128, 512], F32, 'pA')
            pA2 = S.ps([128, 512], F32, 'pA2')
            pO = S.ps([128, 512], F32, 'pO')
            pL = S.ps([128, 512], F32, 'pL')
            pM = S.ps([128, 512], F32, 'pM')
            pTb = S.ps([128, 128], BF16, 'pTb')
            onesb = S.sb([128, 128], BF16, 'onesb')
            S.memset('dve', onesb[:], 1.0)
            idb = S.sb([128, 128], BF16, 'idb')
            S.copy('dve', idb[:], ID)
            stage = S.sb([128, 512], F32, 'cstage')
            stage2 = S.sb([128, 512], F32, 'cstage2')
            stgs = [stage, stage2]

            def load_const_bf16(dst, src, width):
                for i, c0 in enumerate(range(0, width, 512)):
                    w = min(512, width - c0)
                    st = stgs[i % 2]
                    S.dma('sp', st[:, 0:w], src[:, c0:c0 + w])
                    S.copy('dve', dst[:, c0:c0 + w], st[:, 0:w])
            mc = S.sb([128, 17 * 128], BF16, 'mc')
            load_const_bf16(mc, c_mc, 17 * 128)
            ov = S.sb([128, NTC * 128], BF16, 'ov')
            load_const_bf16(ov, c_ov, NTC * 128)
            ew = S.sb([128, SEQ], BF16, 'ew')
            load_const_bf16(ew, c_ew, SEQ)
            m3 = S.sb([128, 1024], BF16, 'm3')
            load_const_bf16(m3, c_m3, 1024)
            b0 = S.sb([128, 256], F32, 'b0')
            S.dma('sp', b0[:], c_b0)
            sel = S.sb([24, 24 * 128], F32, 'sel')
            S.dma('sp', sel[:], c_sel)
            cp = S.sb([128, 2, 32], F32, 'cp')
            S.dma('sp', cp[:], cpos)
            qst = [S.sb([128, 512], BF16, 'qst%d' % i) for i in range(2)]
            qvt = [S.sb([128, 4, 128], BF16, 'qvt%d' % i) for i in range(2)]
            ksT = S.sb([128, SEQ], BF16, 'ksT')
            kwT = S.sb([128, SEQ], BF16, 'kwT')
            kcT = S.sb([128, SEQ], BF16, 'kcT')
            vs = S.sb([128, NT, 128], BF16, 'vs')
            vw = S.sb([128, NT, 128], BF16, 'vw')
            kcb = S.sb([128, NTC * 128], BF16, 'kcb')
            vcb = S.sb([128, NTC, 128], BF16, 'vcb')
            hid = S.sb([128, 2, 512], BF16, 'hid')
            xl = [S.sb([128, 512], BF16, 'xl%d' % i) for i in range(2)]
            w1 = S.sb([128, 32, 256], BF16, 'w1')
            w2 = S.sb([128, 2, 128], BF16, 'w2')
            ldq = [S.sb([128, 512], F32, 'ldq%d' % i) for i in range(2)]
            ldp = [S.sb([128, 512], I32, 'ldp%d' % i) for i in range(2)]
            Pc = [S.sb([128, 512], BF16, 'Pc%d' % i) for i in range(4)]
            Pt = [S.sb([128, 512], BF16, 'Pt%d' % i) for i in range(2)]
            linv = S.sb([128, 512], F32, 'linv')
            otile = S.sb([128, 512], F32, 'otile')
            otmp = S.sb([128, 512], F32, 'otmp')
            gq = S.sb([24, 128], F32, 'gq')
            sc = S.sb([128, 128], F32, 'sc')
            sc2 = S.sb([128, 128], F32, 'sc2')
            m8 = S.sb([128, 8], F32, 'm8')
            selm = S.sb([128, 128], F32, 'selm')
            negm = S.sb([128, 4, 128], BF16, 'negm')
            scale = float(HD ** -0.5)
            cnt = [0]

            def mk_loader(ch):
                def ld(s0, w):
                    cnt[0] += 1
                    t = ldq[cnt[0] % 2]
                    S.dma('sp', t[:, 0:w], projT[ch * 128:(ch + 1) * 128, s0:s0 + w], reads=[('proj', ch)])
                    return t[:, 0:w]
                return ld

            def ld_pos(s0, w):
                cnt[0] += 1
                t = ldp[cnt[0] % 2]
                S.dma('sp', t[:, 0:w], pos[:, s0:s0 + w])
                return t[:, 0:w]

            def tok_major(dst, srcT_bf16):
                for t in range(NT):
                    S.tr(pTb[:], srcT_bf16[:, t * 128:(t + 1) * 128], idb[:])
                    S.copy('act' if t % 2 == 0 else 'dve', dst[:, t, :], pTb[:])

            for gl in range(2):
                for h in range(4):
                    def qdst(s0, w):
                        cnt[0] += 1
                        return qst[cnt[0] % 2][:, 0:w]

                    def qpost(s0, w, ap, h=h):
                        S.dma('sp', qsc[(gl * 4 + h) * 128:(gl * 4 + h + 1) * 128, s0:s0 + w], ap, writes=[('qsc', gl * 4 + h)])
                    rms_rope_stream(S, K, mk_loader(gl * 4 + h), qdst, gv[:, 0:1], ld_pos, SEQ, cst, qpost)
                rms_rope_stream(S, K, mk_loader(12 + gl), lambda s0, w: ksT[:, s0:s0 + w], gv[:, 2:3], ld_pos, SEQ, cst)
                rms_rope_stream(S, K, mk_loader(16 + gl), lambda s0, w: kwT[:, s0:s0 + w], gv[:, 3:4], ld_pos, SEQ, cst)
                for (ch, dst) in ((14 + gl, vs), (18 + gl, vw)):
                    ldr = mk_loader(ch)
                    for s0 in range(0, SEQ, 512):
                        S.copy('act', kcT[:, s0:s0 + 512], ldr(s0, 512))
                    tok_major(dst, kcT)
                for kv in range(2):
                    if kv == 0:
                        rms_rope_stream(S, K, mk_loader(8 + gl), lambda s0, w: kcT[:, s0:s0 + w], gv[:, 1:2], ld_pos, SEQ, cst)
                    else:
                        ldr = mk_loader(10 + gl)
                        for s0 in range(0, SEQ, 512):
                            S.copy('act', kcT[:, s0:s0 + 512], ldr(s0, 512))
                    w1v = cw1[kv].rearrange("(l p) j -> p l j", p=128)
                    S.dma('pool', w1[:, 0:16, :], w1v[:, 0:16, :])
                    S.dma('pool', w1[:, 16:32, :], w1v[:, 16:32, :])
                    S.dma('pool', w2[:], cw2[kv].rearrange("(c p) d -> p c d", p=128))
                    kview = kcT[:].rearrange("p (m s) -> p s m", s=16)
                    for jc in range(2):
                        for l in range(32):
                            x_ = xl[l % 2]
                            S.ts('dve' if l % 2 == 0 else 'pool', x_[:, 0:NCMP], kview[:, l % 16, l // 16:l // 16 + NCMP], cp[:, kv, l:l + 1], None, ALU.add)
                            S.mm(pM[:, 0:NCMP], w1[:, l, jc * 128:(jc + 1) * 128], x_[:, 0:NCMP], start=(l == 0), stop=(l == 31))
                        S.act(hid[:, jc, 0:NCMP], pM[:, 0:NCMP], AF.Silu)
                    if kv == 0:
                        for jc in range(2):
                            S.mm(pM[:, 0:NCMP], w2[:, jc, :], hid[:, jc, 0:NCMP], start=(jc == 0), stop=(jc == 1))
                        S.memset('dve', kcb[:], 0.0)
                        S.copy('act', kcb[:, 0:NCMP], pM[:, 0:NCMP])
                    else:
                        S.memset('dve', vcb[:], 0.0)
                        for t in range(NTC):
                            nn = min(128, NCMP - t * 128)
                            for jc in range(2):
                                S.mm(pM[0:nn, 0:128], hid[:, jc, t * 128:t * 128 + nn], w2[:, jc, :], start=(jc == 0), stop=(jc == 1))
                            S.copy('act', vcb[0:nn, t, :], pM[0:nn, 0:128])
                for qt in range(NT):
                    qs = slice(qt * 128, (qt + 1) * 128)
                    qv_t = qvt[qt % 2]
                    S.dma('sp', qv_t[:], qsc[gl * 512:(gl + 1) * 512, qs].rearrange("(h d) q -> d h q", d=128), reads=[('qsc', gl * 4 + h_) for h_ in range(4)])
                    qv = qv_t[:]
                    S.dma('sp', gq[:], projT[2560:2584, qs], reads=[('proj', 20)])

                    def gate_mul(dst_first, h_branch_src, br):
                        pass
                    ntl = qt // 16
                    tiles = list(range(0, ntl + 1))
                    for idx, t in enumerate(tiles):
                        pa = pA if idx % 2 == 0 else pA2
                        S.mm(pa[:].rearrange("p (h q) -> p h q", h=4), kcb[:, t * 128:(t + 1) * 128], qv)
                        S.act(Pc[idx][:], pa[:], AF.Exp, scale=scale)
                        j = qt - 16 * t
                        if j <= 16:
                            for h in range(4):
                                S.tt('pool', Pc[idx][:, h * 128:(h + 1) * 128], Pc[idx][:, h * 128:(h + 1) * 128], mc[:, j * 128:(j + 1) * 128], ALU.mult)
                    for idx, t in enumerate(tiles):
                        S.mm(pL[:], onesb[:], Pc[idx][:], start=(idx == 0), stop=(idx == len(tiles) - 1))
                    S.ts('dve', linv[:], pL[:], 1e-30, None, ALU.max)
                    S.recip(linv[:], linv[:])
                    for idx, t in enumerate(tiles):
                        S.tt('dve', Pc[idx][:], Pc[idx][:], linv[:], ALU.mult)
                    for idx, t in enumerate(tiles):
                        S.mm(pO[:], vcb[:, t, :], Pc[idx][:], start=(idx == 0), stop=(idx == len(tiles) - 1))
                    n_acc = len(tiles) * 4
                    a = 0
                    for idx, t in enumerate(tiles):
                        for h in range(4):
                            S.mm(pM[:, 0:128], Pc[idx][:, h * 128:(h + 1) * 128], ov[:, t * 128:(t + 1) * 128], start=(a == 0), stop=(a == n_acc - 1))
                            a += 1
                    for h in range(4):
                        j = (gl * 4 + h) * 3 + 0
                        pa = pA if h % 2 == 0 else pA2
                        S.mm(pa[:, 0:128], sel[:, j * 128:(j + 1) * 128], gq[:])
                        S.copy('act', otmp[:, h * 128:(h + 1) * 128], pa[:, 0:128])
                    S.tt('dve', otile[:], pO[:], otmp[:], ALU.mult)
                    S.tt('dve', sc[:], pM[:, 0:128], b0[:, 128 - 2 * qt:256 - 2 * qt], ALU.add)
                    S.ts('dve', sc[:, 0:1], sc[:, 0:1], 1000.0, None, ALU.add)
                    S.op('dve', lambda e: e.max(out=m8[:], in_=sc[:]), reads=[sc], writes=[m8])
                    S.op('dve', lambda e: e.match_replace(out=sc2[:], in_to_replace=m8[:], in_values=sc[:], imm_value=-3.0e38),
                         reads=[m8, sc], writes=[sc2])
                    S.op('dve', lambda e: e.max(out=m8[:], in_=sc2[:]), reads=[sc2], writes=[m8])
                    S.ts('dve', selm[:], sc[:], m8[:, 7:8], None, ALU.is_ge)
                    S.tr(pM[:, 0:128], selm[:], ID)
                    for h in range(4):
                        S.ts('dve', negm[:, h, :], pM[:, 0:128], -1.0, NEGM, ALU.add, ALU.mult)
                    for kt in range(qt + 1):
                        pa = pA if kt % 2 == 0 else pA2
                        P = Pt[kt % 2]
                        S.mm(pa[:].rearrange("p (h q) -> p h q", h=4), ksT[:, kt * 128:(kt + 1) * 128], qv, start=True, stop=False)
                        S.mm(pa[:], ew[:, kt * 128:(kt + 1) * 128], negm[:].rearrange("p h q -> p (h q)"), start=False, stop=True)
                        S.act(P[:], pa[:], AF.Exp, scale=scale)
                        if kt == qt:
                            S.tt('pool', P[:], P[:], m3[:, 0:512], ALU.mult)
                        S.mm(pO[:], vs[:, kt, :], P[:], start=(kt == 0), stop=(kt == qt))
                        S.mm(pL[:], onesb[:], P[:], start=(kt == 0), stop=(kt == qt))
                    S.recip(linv[:], pL[:])
                    S.tt('dve', linv[:], linv[:], pO[:], ALU.mult)
                    for h in range(4):
                        j = (gl * 4 + h) * 3 + 1
                        pa = pA if h % 2 == 0 else pA2
                        S.mm(pa[:, 0:128], sel[:, j * 128:(j + 1) * 128], gq[:])
                        S.copy('act', otmp[:, h * 128:(h + 1) * 128], pa[:, 0:128])
                    S.tt('dve', otmp[:], otmp[:], linv[:], ALU.mult)
                    S.tt('dve', otile[:], otile[:], otmp[:], ALU.add)
                    kts = [kt for kt in range(qt - 4, qt + 1) if kt >= 0]
                    for idx, kt in enumerate(kts):
                        pa = pA if idx % 2 == 0 else pA2
                        P = Pt[idx % 2]
                        S.mm(pa[:].rearrange("p (h q) -> p h q", h=4), kwT[:, kt * 128:(kt + 1) * 128], qv)
                        S.act(P[:], pa[:], AF.Exp, scale=scale)
                        if kt == qt:
                            S.tt('pool', P[:], P[:], m3[:, 0:512], ALU.mult)
                        elif kt == qt - 4:
                            S.tt('pool', P[:], P[:], m3[:, 512:1024], ALU.mult)
                        S.mm(pO[:], vw[:, kt, :], P[:], start=(idx == 0), stop=(idx == len(kts) - 1))
                        S.mm(pL[:], onesb[:], P[:], start=(idx == 0), stop=(idx == len(kts) - 1))
                    S.recip(linv[:], pL[:])
                    S.tt('dve', linv[:], linv[:], pO[:], ALU.mult)
                    for h in range(4):
                        j = (gl * 4 + h) * 3 + 2
                        pa = pA if h % 2 == 0 else pA2
                        S.mm(pa[:, 0:128], sel[:, j * 128:(j + 1) * 128], gq[:])
                        S.copy('act', otmp[:, h * 128:(h + 1) * 128], pa[:, 0:128])
                    S.tt('dve', otmp[:], otmp[:], linv[:], ALU.mult)
                    S.tt('dve', otile[:], otile[:], otmp[:], ALU.add)
                    S.dma('sp', mixT[gl * 512:(gl + 1) * 512, qs].rearrange("(h d) q -> d h q", d=128),
                          otile[:].rearrange("p (h q) -> p h q", h=4))
            S.barrier()
        S.finish()
        S.emit()
    return nc


def _g16(v):
    return np.ascontiguousarray(np.asarray(v, np.float32).reshape(16, 128).T)


def _even_inputs(x_bT, pos_b, hh, gmix, w_in, conv_w, a_log, dt_bias, gdn_norm, qn, kn):
    cols = []
    for base in (0, 1024, 2048, 3072):
        cols.append(np.arange(base + 512 * hh, base + 512 * hh + 512))
    for base in (4112, 5136, 6160):
        cols.append(np.arange(base + 512 * hh, base + 512 * hh + 512))
    cols.append(np.arange(4096 + 4 * hh, 4096 + 4 * hh + 4))
    cols.append(np.arange(4104 + 4 * hh, 4104 + 4 * hh + 4))
    cols = np.concatenate(cols)
    cw = np.zeros((128, 12, 4), np.float32)
    for i, base in enumerate((0, 1024, 2048)):
        for j in range(4):
            ch = base + 512 * hh + j * 128
            cw[:, i * 4 + j, :] = conv_w[:, ch:ch + 128].T
    gvec = np.zeros((128, 16), np.float32)
    gvec[:, 0:4] = a_log[4 * hh:4 * hh + 4][None, :]
    gvec[:, 4:8] = dt_bias[4 * hh:4 * hh + 4][None, :]
    gvec[:, 8] = gdn_norm
    gvec[:, 9] = qn
    gvec[:, 10] = kn
    return {"xT": x_bT, "gmix": _g16(gmix), "w_in": np.ascontiguousarray(w_in[:, cols]), "convw": cw, "gvec": gvec,
            "pos": np.ascontiguousarray(np.broadcast_to(pos_b[None, :], (128, len(pos_b)))).astype(np.int32), "consts": make_consts()}


def _odd_inputs(x_bT, pos_b, hh, gmix, w_in, qn, kn, cpos, cw1, cw2, SEQ, co):
    cols = [np.arange(1024 * hh, 1024 * hh + 1024)]
    for base in (2048, 2560, 3072, 3584, 4096, 4608):
        cols.append(np.arange(base + 256 * hh, base + 256 * hh + 256))
    cols.append(np.arange(5120 + 24 * hh, 5120 + 24 * hh + 24))
    cols = np.concatenate(cols)
    gvec = np.zeros((128, 8), np.float32)
    gvec[:, 0] = qn
    gvec[:, 1] = kn[0]
    gvec[:, 2] = kn[1]
    gvec[:, 3] = kn[2]
    return {"xT": x_bT, "gmix": _g16(gmix), "w_in": np.ascontiguousarray(w_in[:, cols]), "gvec": gvec,
            "cpos": np.ascontiguousarray(np.asarray(cpos).transpose(2, 0, 1)), "cw1": np.ascontiguousarray(cw1), "cw2": np.ascontiguousarray(cw2),
            "pos": np.ascontiguousarray(np.broadcast_to(pos_b[None, :], (128, len(pos_b)))).astype(np.int32),
            "consts": make_consts(), "c_mc": co['mc'], "c_ov": co['ov'], "c_b0": co['b0'], "c_ew": co['ew'], "c_m3": co['m3'], "c_sel": co['sel']}


_PROGS = {}


def _prog(key, fn):
    if key not in _PROGS:
        _PROGS[key] = fn()
    return _PROGS[key]


def kernel(**inp):
    inp = {k: np.asarray(v) for k, v in inp.items()}
    x = inp['x']
    mem = inp['mem']
    pos = inp['positions']
    B, SEQ, _ = x.shape
    T = SEQ // 2
    DEPTH = inp['ffn1_norm'].shape[0]
    xT = [np.ascontiguousarray(x[b].T) for b in range(B)]
    memT = [np.ascontiguousarray(mem[b].T) for b in range(B)]
    cores = list(range(8))

    def run_row(mode, i, mixT):
        nc = _prog(('row', mode), lambda: build_row(T, mode))
        ims = []
        for c in cores:
            b, half = c // 2, c % 2
            sl = slice(half * T, (half + 1) * T)
            gains = np.zeros((128, 4, 16), np.float32)
            im = {"xT": np.ascontiguousarray(xT[b][:, sl])}
            if mode == 'first':
                gains[:, 2] = _g16(inp['ffn1_norm'][0])
                im["w_gu0"] = inp['ffn1_w_gu'][0]
                im["w_down0"] = inp['ffn1_w_down'][0]
            else:
                gains[:, 0] = _g16(inp['xa_norm'][i])
                gains[:, 1] = _g16(inp['xa_mem_norm'][i])
                gains[:, 2] = _g16(inp['ffn2_norm'][i])
                im["w_gu0"] = inp['ffn2_w_gu'][i]
                im["w_down0"] = inp['ffn2_w_down'][i]
                if mode == 'mid':
                    gains[:, 3] = _g16(inp['ffn1_norm'][i + 1])
                    im["w_gu1"] = inp['ffn1_w_gu'][i + 1]
                    im["w_down1"] = inp['ffn1_w_down'][i + 1]
                im["mixT"] = np.ascontiguousarray(mixT[b][:, sl])
                im["memT"] = memT[b]
                im["w_out"] = inp['ev_w_out'][i // 2] if i % 2 == 0 else inp['od_w_out'][i // 2]
                im["w_q"] = inp['xa_w_q'][i]
                im["w_kv"] = inp['xa_w_kv'][i]
                im["w_o"] = inp['xa_w_o'][i]
                xg = np.zeros((128, 2), np.float32)
                xg[:, 0] = inp['xa_q_norm'][i]
                xg[:, 1] = inp['xa_k_norm'][i]
                im["xgain"] = xg
            im["gains"] = gains
            ims.append(im)
        res = run_bass_kernel_spmd(nc, ims, core_ids=cores)
        for c in cores:
            b, half = c // 2, c % 2
            xT[b][:, half * T:(half + 1) * T] = res.results[c]["yT"]

    def run_mix(i):
        mixT = [np.zeros((2048, SEQ), np.float32) for _ in range(B)]
        if i % 2 == 0:
            e = i // 2
            nc = _prog('even', lambda: build_even(SEQ))
            ims = [_even_inputs(xT[c // 2], pos[c // 2], c % 2, inp['mix_norm'][i], inp['ev_w_in'][e], inp['gdn_conv_w'][e], inp['gdn_a_log'][e],
                                inp['gdn_dt_bias'][e], inp['gdn_out_norm'][e], inp['dsw_q_norm'][e], inp['dsw_k_norm'][e]) for c in cores]
            res = run_bass_kernel_spmd(nc, ims, core_ids=cores)
            for c in cores:
                b, hh = c // 2, c % 2
                m = res.results[c]["mixT"]
                mixT[b][512 * hh:512 * hh + 512] = m[0:512]
                mixT[b][1024 + 512 * hh:1024 + 512 * hh + 512] = m[512:1024]
        else:
            o = i // 2
            nc = _prog('odd', lambda: build_odd(SEQ))
            co = make_consts_odd(SEQ)
            ims = [_odd_inputs(xT[c // 2], pos[c // 2], c % 2, inp['mix_norm'][i], inp['od_w_in'][o], inp['nsa_q_norm'][o], inp['nsa_k_norm'][o],
                               inp['nsa_cmp_pos'][o], inp['nsa_cmp_w1'][o], inp['nsa_cmp_w2'][o], SEQ, co) for c in cores]
            res = run_bass_kernel_spmd(nc, ims, core_ids=cores)
            for c in cores:
                b, hh = c // 2, c % 2
                mixT[b][1024 * hh:1024 * hh + 1024] = res.results[c]["mixT"]
        return mixT

    run_row('first', 0, None)
    for i in range(DEPTH):
        mixT = run_mix(i)
        run_row('mid' if i + 1 < DEPTH else 'last', i, mixT)
    out = np.stack([np.ascontiguousarray(xT[b].T) for b in range(B)]).astype(np.float32)
    return out
```
